# Optimizing a Trainium2 kernel written in Bass

```python
import math
import jax, jax.numpy as jnp
from jax import lax
import numpy as np

D_MODEL = 1024
BATCH = 8
SEQ = 2048
DEPTH = 2

HEAD_DIM = 64
N_HEADS_A = 8
N_HEADS_B = 8
N_KV_HEADS_B = 2
DILATED_BRANCHES = ((128, 1), (512, 4), (2048, 16))
WINDOW_B = 128
REL_BUCKETS = 32
REL_MAX_DIST = 1024
N_REL_HEADS = N_HEADS_A + N_HEADS_B
D_ATTN_IN = (3 * N_HEADS_A + N_HEADS_B + 2 * N_KV_HEADS_B) * HEAD_DIM
D_ATTN_OUT = (N_HEADS_A + N_HEADS_B) * HEAD_DIM
RWKV_HEAD = 64
RWKV_HEADS = D_MODEL // RWKV_HEAD
DECAY_LORA = 64
AAA_LORA = 64
GATE_LORA = 128
D_FF = 4 * D_MODEL
N_ATTN_LAYERS = (DEPTH + 1) // 2
N_RWKV_LAYERS = DEPTH // 2
NORM_EPS = 1e-6
GN_EPS = 64e-5
NEG_INF = -1e30

kernel_name = 'hybrid_dilated_swa_rwkv7_encoder'

F32 = jnp.float32


def rms_norm(x, g):
    x32 = x.astype(F32)
    y = x32 * lax.rsqrt(jnp.mean(x32 * x32, axis=-1, keepdims=True) + NORM_EPS)
    return (y * g.astype(F32)).astype(x.dtype)


def t5_bucket(rel):
    nb = REL_BUCKETS // 2
    max_exact = nb // 2
    bucket = jnp.where(rel > 0, nb, 0)
    n = jnp.abs(rel)
    nf = jnp.maximum(n, 1).astype(F32)
    large = max_exact + (jnp.log(nf / max_exact) / math.log(REL_MAX_DIST / max_exact)
                         * (nb - max_exact)).astype(jnp.int32)
    large = jnp.minimum(large, nb - 1)
    return bucket + jnp.where(n < max_exact, n, large)


def band_rel_bias(table_h, block, dilation):
    i = jnp.arange(block)[:, None]
    j = jnp.arange(3 * block)[None, :]
    bucket = t5_bucket((j - block - i) * dilation)
    return jnp.transpose(table_h[bucket], (2, 0, 1))


def banded_attention(q, k, v, bias, radius, block, sink=None):
    nB, H, G, L, hd = q.shape
    nb = -(-L // block)
    pad = nb * block - L
    qb = jnp.pad(q, [(0, 0)] * 3 + [(0, pad), (0, 0)]).reshape(nB, H, G, nb, block, hd)

    def windows(t):
        tb = jnp.pad(t, [(0, 0)] * 3 + [(block, block + pad), (0, 0)]).reshape(nB, H, G, nb + 2, block, hd)
        return jnp.concatenate([tb[:, :, :, :-2], tb[:, :, :, 1:-1], tb[:, :, :, 2:]], axis=-2)

    kb, vb = windows(k), windows(v)
    valid = jnp.pad(jnp.ones((L,), bool), (block, block + pad)).reshape(nb + 2, block)
    valid = jnp.concatenate([valid[:-2], valid[1:-1], valid[2:]], axis=-1)
    rel = jnp.arange(3 * block)[None, :] - block - jnp.arange(block)[:, None]
    mask = (jnp.abs(rel) <= radius)[None] & valid[:, None, :]
    s = jnp.einsum('bhgnqd,bhgnkd->bhgnqk', qb, kb).astype(F32) * (hd ** -0.5)
    s = s + bias.astype(F32)[None, :, None, None]
    s = jnp.where(mask, s, NEG_INF)
    m = jnp.max(s, axis=-1)
    if sink is not None:
        sink_b = sink.astype(F32)[None, :, None, None, None]
        m = jnp.maximum(m, sink_b)
    p = jnp.exp(s - m[..., None])
    denom = jnp.sum(p, axis=-1)
    if sink is not None:
        denom = denom + jnp.exp(sink_b - m)
    o = jnp.einsum('bhgnqk,bhgnkd->bhgnqd', p, vb.astype(F32)) / denom[..., None]
    lse = m + jnp.log(denom)
    o = o.reshape(nB, H, G, nb * block, hd)[..., :L, :]
    lse = lse.reshape(nB, H, G, nb * block)[..., :L]
    return o, lse


def dilated_attention(q, k, v, table_a):
    nB, H, S, hd = q.shape
    outs, lses = [], []
    for window, dil in DILATED_BRANCHES:
        radius = window // 2 // dil

        def split(t):
            return t.reshape(nB, H, S // dil, dil, hd).transpose(0, 1, 3, 2, 4)

        bias = band_rel_bias(table_a, radius, dil)
        o, lse = banded_attention(split(q), split(k), split(v), bias, radius, radius)
        outs.append(o.transpose(0, 1, 3, 2, 4).reshape(nB, H, S, hd))
        lses.append(lse.transpose(0, 1, 3, 2).reshape(nB, H, S))
    wts = jax.nn.softmax(jnp.stack(lses), axis=0)
    return jnp.einsum('rbhs,rbhsd->bhsd', wts, jnp.stack(outs))


def window_gqa(q, k, v, sink, table_b):
    rep = N_HEADS_B // N_KV_HEADS_B
    k = jnp.repeat(k, rep, axis=1)
    v = jnp.repeat(v, rep, axis=1)
    bias = band_rel_bias(table_b, WINDOW_B, 1)
    o, _ = banded_attention(q[:, :, None], k[:, :, None], v[:, :, None], bias, WINDOW_B, WINDOW_B, sink)
    return o[:, :, 0]


def attention_mixer(h, w_in, sink, w_out, rel_table):
    nB, S, _ = h.shape
    proj = h @ w_in
    sizes = [N_HEADS_A * HEAD_DIM] * 3 + [N_HEADS_B * HEAD_DIM, N_KV_HEADS_B * HEAD_DIM, N_KV_HEADS_B * HEAD_DIM]
    cuts = [int(c) for c in np.cumsum(sizes)[:-1]]
    qa, ka, va, qb, kb, vb = jnp.split(proj, cuts, axis=-1)

    def heads(t):
        return t.reshape(nB, S, -1, HEAD_DIM).transpose(0, 2, 1, 3)

    oa = dilated_attention(heads(qa), heads(ka), heads(va), rel_table[:, :N_HEADS_A])
    ob = window_gqa(heads(qb), heads(kb), heads(vb), sink, rel_table[:, N_HEADS_A:])
    o = jnp.concatenate([oa, ob], axis=1).astype(h.dtype)
    o = o.transpose(0, 2, 1, 3).reshape(nB, S, D_ATTN_OUT)
    return o @ w_out


def rwkv7_scan(r, w, k, v, kk, a):
    def step(state, inp):
        r_t, w_t, k_t, v_t, kk_t, a_t = inp
        sa = jnp.einsum('bhvk,bhk->bhv', state, -kk_t)
        state = (state * w_t[:, :, None, :] + sa[..., None] * (kk_t * a_t)[:, :, None, :]
                 + v_t[..., None] * k_t[:, :, None, :])
        return state, jnp.einsum('bhvk,bhk->bhv', state, r_t)

    nB, S, H, N = r.shape
    xs = tuple(jnp.moveaxis(t, 1, 0) for t in (r, w, k, v, kk, a))
    _, out = lax.scan(step, jnp.zeros((nB, H, N, N), F32), xs)
    return jnp.moveaxis(out, 0, 1)


def group_norm(o, gn_w, gn_b):
    mu = jnp.mean(o, axis=-1, keepdims=True)
    var = jnp.mean(jnp.square(o - mu), axis=-1, keepdims=True)
    y = (o - mu) * lax.rsqrt(var + GN_EPS)
    nB, S, H, N = o.shape
    return y.reshape(nB, S, H * N) * gn_w.astype(F32) + gn_b.astype(F32)


def rwkv7_mixer(h, mu_prev, mu_next, w_r, w_k, w_v, w_o, k_k, k_a, r_k, gn_w, gn_b,
                w0, w1, w2, a0, a1, a2, g1, g2):
    nB, S, D = h.shape
    H, N = RWKV_HEADS, RWKV_HEAD
    dx_p = jnp.pad(h, ((0, 0), (1, 0), (0, 0)))[:, :-1] - h
    dx_n = jnp.pad(h, ((0, 0), (0, 1), (0, 0)))[:, 1:] - h

    def mix(c):
        return h + dx_p * mu_prev[c] + dx_n * mu_next[c]

    def heads(t):
        return t.astype(F32).reshape(nB, S, H, N)

    xw, xa, xg = mix(1), mix(4), mix(5)
    r = mix(0) @ w_r
    k = mix(2) @ w_k
    v = mix(3) @ w_v
    kk = heads(k * k_k)
    kk = kk / jnp.maximum(jnp.sqrt(jnp.sum(kk * kk, axis=-1, keepdims=True)), 1e-12)
    rh, vh = heads(r), heads(v)
    dir_out = []
    for d in range(2):
        w_log = -jax.nn.softplus(-(w0[d] + jnp.tanh(xw @ w1[d]) @ w2[d])) - 0.5
        decay = jnp.exp(-jnp.exp(w_log.astype(F32)))
        a = jax.nn.sigmoid(a0[d] + (xa @ a1[d]) @ a2[d])
        g = jax.nn.sigmoid(xg @ g1[d]) @ g2[d]
        kd = heads(k * (1 + (a - 1) * k_a))
        ins = [rh, heads(decay), kd, vh, kk, heads(a)]
        if d == 1:
            ins = [jnp.flip(t, axis=1) for t in ins]
        o = rwkv7_scan(*ins)
        if d == 1:
            o = jnp.flip(o, axis=1)
        bonus = jnp.sum(rh * kd * r_k.astype(F32), axis=-1, keepdims=True) * vh
        dir_out.append((group_norm(o, gn_w, gn_b) + bonus.reshape(nB, S, D)) * g.astype(F32))
    y = dir_out[0] + dir_out[1]
    return y.astype(h.dtype) @ w_o


def setup_inputs(seed: int = 0) -> dict:
    key = jax.random.key(seed)
    ks = iter(jax.random.split(key, 40))

    def nrm(shape, scale):
        return scale * jax.random.normal(next(ks), shape, F32)

    NA, NR, D = N_ATTN_LAYERS, N_RWKV_LAYERS, D_MODEL
    return {
        'x': nrm((BATCH, SEQ, D), 1.0),
        'rel_table': nrm((REL_BUCKETS, N_REL_HEADS), 0.5),
        'norm_g': 1.0 + nrm((DEPTH, 4, D), 0.05),
        'attn_w_in': nrm((NA, D, D_ATTN_IN), D ** -0.5),
        'attn_sink': nrm((NA, N_HEADS_B), 0.5),
        'attn_w_out': nrm((NA, D_ATTN_OUT, D), D_ATTN_OUT ** -0.5),
        'rk_mu_prev': jax.random.uniform(next(ks), (NR, 6, D), F32, 0.0, 0.5),
        'rk_mu_next': jax.random.uniform(next(ks), (NR, 6, D), F32, 0.0, 0.5),
        'rk_w_r': nrm((NR, D, D), D ** -0.5),
        'rk_w_k': nrm((NR, D, D), D ** -0.5),
        'rk_w_v': nrm((NR, D, D), D ** -0.5),
        'rk_w_o': nrm((NR, D, D), D ** -0.5),
        'rk_k_k': 0.85 + nrm((NR, D), 0.05),
        'rk_k_a': 1.0 + nrm((NR, D), 0.05),
        'rk_r_k': nrm((NR, RWKV_HEADS, RWKV_HEAD), 0.1),
        'rk_gn_w': 1.0 + nrm((NR, D), 0.05),
        'rk_gn_b': nrm((NR, D), 0.02),
        'rk_w0': -2.0 + nrm((NR, 2, D), 1.0),
        'rk_w1': nrm((NR, 2, D, DECAY_LORA), D ** -0.5),
        'rk_w2': nrm((NR, 2, DECAY_LORA, D), 0.1 * DECAY_LORA ** -0.5),
        'rk_a0': nrm((NR, 2, D), 0.5),
        'rk_a1': nrm((NR, 2, D, AAA_LORA), D ** -0.5),
        'rk_a2': nrm((NR, 2, AAA_LORA, D), 0.5 * AAA_LORA ** -0.5),
        'rk_g1': nrm((NR, 2, D, GATE_LORA), D ** -0.5),
        'rk_g2': nrm((NR, 2, GATE_LORA, D), GATE_LORA ** -0.5),
        'mlp_w1': nrm((DEPTH, D, D_FF), D ** -0.5),
        'mlp_w2': nrm((DEPTH, D_FF, D), D_FF ** -0.5),
    }


def reference(x, rel_table, norm_g, attn_w_in, attn_sink, attn_w_out,
              rk_mu_prev, rk_mu_next, rk_w_r, rk_w_k, rk_w_v, rk_w_o, rk_k_k, rk_k_a, rk_r_k,
              rk_gn_w, rk_gn_b, rk_w0, rk_w1, rk_w2, rk_a0, rk_a1, rk_a2, rk_g1, rk_g2,
              mlp_w1, mlp_w2):
    h = x
    for layer in range(DEPTH):
        g = norm_g[layer]
        i = layer // 2
        u = rms_norm(h, g[0])
        if layer % 2 == 0:
            u = attention_mixer(u, attn_w_in[i], attn_sink[i], attn_w_out[i], rel_table)
        else:
            u = rwkv7_mixer(u, rk_mu_prev[i], rk_mu_next[i], rk_w_r[i], rk_w_k[i], rk_w_v[i], rk_w_o[i],
                            rk_k_k[i], rk_k_a[i], rk_r_k[i], rk_gn_w[i], rk_gn_b[i],
                            rk_w0[i], rk_w1[i], rk_w2[i], rk_a0[i], rk_a1[i], rk_a2[i], rk_g1[i], rk_g2[i])
        h = h + rms_norm(u, g[1])
        u = rms_norm(h, g[2])
        u = jnp.square(jax.nn.relu(u @ mlp_w1[layer])) @ mlp_w2[layer]
        h = h + rms_norm(u, g[3])
    return h
```

```python
import math
from contextlib import ExitStack

import numpy as np
import concourse.bass as bass
import concourse.mybir as mybir
from concourse.bass_utils import run_bass_kernel_spmd

F32 = mybir.dt.float32
BF16 = mybir.dt.bfloat16
AF = mybir.ActivationFunctionType
ALU = mybir.AluOpType

P = 128
S = 2048
D = 1024
KC = 8
DFF = 4096
NEG = -30000.0
EPS = 1e-6
GN_EPS = 64e-5
SEM_CH = 20000


class Res:
    __slots__ = ("name", "w", "readers", "dsem", "dval")

    def __init__(self, name):
        self.name = name
        self.w = None
        self.readers = {}
        self.dsem = None
        self.dval = 0


class KB:
    def __init__(self, nc, es):
        self.nc = nc
        self.es = es
        self.eng = {"pe": nc.tensor, "act": nc.scalar, "dve": nc.vector, "pool": nc.gpsimd, "sp": nc.sync}
        self.cnt = {e: 0 for e in self.eng}
        self.sems = {e: [] for e in self.eng}
        self.seen = {e: {} for e in self.eng}
        self.dres = []
        self.nsem = 0
        self.bank_i = 0
        self.held = set()
        self.pend = {}
        self.banks = []
        for i in range(8):
            t = es.enter_context(nc.psum_tensor(f"bank{i}", [P, 512], F32))
            self.banks.append((t[:, :], Res(f"bank{i}")))

    def new_sem(self, name):
        self.nsem += 1
        return self.es.enter_context(self.nc.semaphore(f"{name}_{self.nsem}"))

    def bank(self, hold=False):
        while (self.bank_i % 8) in self.held:
            self.bank_i += 1
        i = self.bank_i % 8
        self.bank_i += 1
        if hold:
            self.held.add(i)
        return self.banks[i]

    def release(self, bank_ap_res):
        for i, b in enumerate(self.banks):
            if b[1] is bank_ap_res:
                self.held.discard(i)

    def _deps(self, e, r, w, x):
        toks = []
        for res in r:
            if res.w is not None:
                toks.append(res.w[0])
        strict = e != "pe"
        for res in x:
            if res.w is not None:
                toks.append(res.w[0])
            for k, tok in res.readers.items():
                if k != e:
                    toks.append(tok)
        for res in w:
            if res.w is not None:
                if res.w[1] != e:
                    toks.append(res.w[0])
            for k, tok in res.readers.items():
                if k != e or strict:
                    toks.append(tok)
        return toks

    def _wait(self, e, toks):
        seen = self.seen[e]
        eng = self.eng[e]
        for tok in toks:
            sem, val, key = tok[0], tok[1], tok[2]
            if seen.get(key, 0) < val:
                eng.wait_ge(sem, val)
                seen[key] = val
            if len(tok) > 3:
                for k2, v2 in tok[3].items():
                    if seen.get(k2, 0) < v2:
                        seen[k2] = v2

    def _record(self, tok, e, r, w, x, rkey=None):
        k = rkey or e
        for res in w:
            res.w = (tok, e)
            res.readers = {}
        for res in r:
            res.readers[k] = tok
        for res in x:
            res.readers[k] = tok

    def op(self, e, fn, r=(), w=(), x=(), inc=True):
        self._wait(e, self._deps(e, r, w, x))
        ins = fn(self.eng[e])
        if not inc:
            pr, pw, px = self.pend.setdefault(e, ([], [], []))
            pr.extend(r); pw.extend(w); px.extend(x)
        if inc:
            if e in self.pend:
                pr, pw, px = self.pend.pop(e)
                r = tuple(dict.fromkeys(list(r) + pr))
                w = tuple(dict.fromkeys(list(w) + pw))
                x = tuple(dict.fromkeys(list(x) + px))
            i = self.cnt[e]
            self.cnt[e] += 1
            si, v = divmod(i, SEM_CH)
            while len(self.sems[e]) <= si:
                self.sems[e].append(self.new_sem(f"s{e}"))
            sem = self.sems[e][si]
            ins.then_inc(sem, 1)
            snap = dict(self.seen[e])
            snap[f"{e}{si}"] = v + 1
            tok = (sem, v + 1, f"{e}{si}", snap)
            self._record(tok, e, r, w, x)
        return ins

    def dma(self, q, out, in_, r=(), w=(), **kw):
        self._wait(q, self._deps("dma", r, w, ()))
        ins = self.eng[q].dma_start(out=out, in_=in_, **kw)
        tgt = w[0]
        if tgt.dsem is None:
            tgt.dsem = self.new_sem("d")
            self.dres.append(tgt)
        tgt.dval += 16
        ins.then_inc(tgt.dsem, 16)
        tok = (tgt.dsem, tgt.dval, f"d{id(tgt)}", dict(self.seen[q]))
        self._record(tok, "dma", r, w, (), rkey=f"dma{id(tgt)}")
        return ins

    def barrier(self):
        toks = []
        for e in self.eng:
            i = self.cnt[e]
            if i == 0:
                continue
            si, v = divmod(i - 1, SEM_CH)
            toks.append((self.sems[e][si], v + 1, f"{e}{si}"))
        for res in self.dres:
            toks.append((res.dsem, res.dval, f"d{id(res)}"))
        for e in self.eng:
            self._wait(e, toks)

    def mmg(self, out, bres, terms, r=(), skip=False, first=True, last=True):
        n = len(terms)
        for i, (l, rh) in enumerate(terms):
            fin = i == n - 1
            kw = {}
            if skip:
                kw["skip_group_check"] = True
            self.op("pe", lambda t, l=l, rh=rh, i=i, fin=fin: t.matmul(
                out, lhsT=l, rhs=rh, start=(first and i == 0), stop=(last and fin), **kw),
                r=r, w=(bres,), inc=fin)


class Arena:
    def __init__(self, nc, es, nbytes):
        self.t = es.enter_context(nc.sbuf_tensor("arena", [P, nbytes // 4], F32))
        self.top = 0
        self.cap = nbytes

    def alloc(self, dtype, shape):
        esz = 4 if dtype == F32 else 2
        n = 1
        for s in shape:
            n *= s
        nb = (n * esz + 63) // 64 * 64
        off = self.top
        self.top += nb
        assert self.top <= self.cap, f"SBUF arena overflow {self.top} > {self.cap}"
        ap = self.t[:, off // 4:(off + nb) // 4]
        if dtype != F32:
            ap = ap.bitcast(dtype)
        ap = ap[:, 0:n]
        if len(shape) == 2:
            ap = ap.rearrange("p (a b) -> p a b", a=shape[0])
        elif len(shape) == 3:
            ap = ap.rearrange("p (a b c) -> p a b c", a=shape[0], b=shape[1])
        return ap


def t5_bucket_np(rel):
    nb = 16
    max_exact = 8
    rel = np.asarray(rel, np.int64)
    bucket = np.where(rel > 0, nb, 0)
    n = np.abs(rel)
    nf = np.maximum(n, 1).astype(np.float32)
    large = max_exact + (np.log(nf / np.float32(max_exact)) / np.float32(math.log(1024 / max_exact))
                         * np.float32(nb - max_exact)).astype(np.int32)
    large = np.minimum(large, nb - 1)
    return bucket + np.where(n < max_exact, n, large)


def colvec(v):
    v = np.asarray(v, np.float32)
    return np.ascontiguousarray(v.reshape(-1, P).T)


def build(stop=None):
    nc = bass.Bass("TRN2", target_bir_lowering=False)
    es = ExitStack()
    dt = lambda name, shape: nc.dram_tensor(name, shape, F32, kind="ExternalInput").ap()
    x_d = dt("x", [S, D])
    biasA_d = dt("biasA", [8, P, 896])
    biasB_d = dt("biasB", [8, P, 384])
    gains_d = dt("gains", [P, 64])
    sink_d = dt("sink", [P, 8])
    cst_d = dt("cst", [P, 4 * P])
    w_in_d = dt("w_in", [D, 2304])
    w_out_d = dt("w_out", [D, D])
    w1_d = dt("mlp_w1", [2, D, DFF])
    w2_d = dt("mlp_w2", [2, DFF, D])
    cst2_d = dt("cst2", [P, 11 * P])
    rkv_d = dt("rkv", [P, 72])
    mu_d = dt("mu", [P, 96])
    wr_d = dt("rk_w_r", [D, D])
    wk_d = dt("rk_w_k", [D, D])
    wv_d = dt("rk_w_v", [D, D])
    wo2_d = dt("rk_w_o", [D, D])
    lora1_d = dt("lora1", [D, 512])
    w2c_d = dt("w2cat", [P, D])
    a2c_d = dt("a2cat", [P, D])
    g2_d = dt("g2", [2, P, D])
    y_d = nc.dram_tensor("y", [S, D], F32, kind="ExternalOutput").ap()

    K = KB(nc, es)
    A = Arena(nc, es, 206 * 1024)

    cst = A.alloc(F32, [4 * P])
    cstb = A.alloc(BF16, [4 * P])
    gains = A.alloc(F32, [64])
    esink = A.alloc(F32, [8])
    epsc = A.alloc(F32, [4])
    R_epsc = Res("epsc")
    R_cst, R_cstb, R_gains, R_esink = Res("cst"), Res("cstb"), Res("gains"), Res("esink")
    K.dma("sp", cst, cst_d, w=(R_cst,))
    K.dma("sp", gains, gains_d, w=(R_gains,))
    K.dma("sp", esink, sink_d, w=(R_esink,))
    K.op("act", lambda e: e.activation(out=esink, in_=esink, func=AF.Exp), r=(R_esink,), w=(R_esink,))
    K.op("dve", lambda e: e.tensor_copy(out=cstb, in_=cst), r=(R_cst,), w=(R_cstb,))
    ident_f = cst[:, 0:P]
    swap_f = cst[:, P:2 * P]
    ident_b = cstb[:, 0:P]
    ones_b = cstb[:, 2 * P:3 * P]

    uT = A.alloc(BF16, [KC, S])
    R_uT = Res("uT")
    X0 = A.top
    rstd = A.alloc(F32, [512])
    R_rstd = Res("rstd")
    sq = A.alloc(BF16, [KC, 512])
    R_sq = Res("sq")
    xin = [A.alloc(F32, [D]) for _ in range(2)]
    R_xin = [Res("xin0"), Res("xin1")]
    X1 = A.top
    WSLOT = 4
    W0 = A.top
    wall = A.alloc(BF16, [WSLOT * 4096])
    W1 = A.top
    wslot = [wall[:, i * 4096:(i + 1) * 4096] for i in range(WSLOT)]
    R_w = [Res(f"w{i}") for i in range(WSLOT)]
    wi = [0]
    mark_hT = A.top
    hT = A.alloc(F32, [KC, S])
    R_hT = Res("hT")
    mark_top = A.top
    OT = A.alloc(BF16, [KC, S])
    R_OT = Res("OT")
    mark_c = A.top

    def getw():
        i = wi[0] % WSLOT
        wi[0] += 1
        return wslot[i], R_w[i]

    def gcol(l, j, c):
        k = (l * 4 + j) * 8 + c
        return gains[:, k:k + 1]

    def rstd_from(ps, bres, n, out, rout, scale=1.0 / D, eps=EPS):
        K.op("act", lambda e: e.activation(out=out, in_=ps, func=AF.Ln, scale=scale, bias=eps_ap(eps)),
             x=(bres,), w=(rout,), r=(R_epsc,))
        K.op("act", lambda e: e.activation(out=out, in_=out, func=AF.Exp, scale=-0.5), r=(rout,), w=(rout,))

    K.op("pool", lambda e: e.memset(epsc[:, 0:1], EPS), w=(R_epsc,))
    K.op("pool", lambda e: e.memset(epsc[:, 1:2], GN_EPS), r=(), w=(R_epsc,))
    K.op("pool", lambda e: e.memset(epsc[:, 2:3], 1e-18), r=(), w=(R_epsc,))

    def eps_ap(eps):
        return epsc[:, 0:1] if eps == EPS else epsc[:, 1:2]

    def load_x_tile(tt):
        b = tt % 2
        K.dma("sp", xin[b], x_d[tt * P:(tt + 1) * P, :], w=(R_xin[b],))
        return xin[b], R_xin[b]

    A.top = mark_hT
    hblk = A.alloc(F32, [KC, 512])
    R_hblk = Res("hblk")

    def xT_block(tb, dst, rdst):
        for t4 in range(4):
            tt = tb * 4 + t4
            xt, rx = load_x_tile(tt)
            for half in range(2):
                ps, rb = K.bank()
                for c4 in range(4):
                    c = half * 4 + c4
                    K.op("pe", lambda e, c=c, c4=c4: e.transpose(ps[:, c4 * P:(c4 + 1) * P], xt[:, c * P:(c + 1) * P], ident_f),
                         r=(rx, R_cst), w=(rb,), inc=(c4 == 3))
                K.op("act" if half == 0 else "dve",
                     (lambda e, half=half, t4=t4: e.activation(out=dst[:, half * 4:half * 4 + 4, t4 * P:(t4 + 1) * P],
                                                               in_=ps.rearrange("p (a b) -> p a b", a=4), func=AF.Copy))
                     if half == 0 else
                     (lambda e, half=half, t4=t4: e.tensor_copy(out=dst[:, half * 4:half * 4 + 4, t4 * P:(t4 + 1) * P],
                                                                in_=ps.rearrange("p (a b) -> p a b", a=4))),
                     x=(rb,), w=(rdst,))

    def prenorm_block(src, rsrc, l, j, dst_uT, ntok=512):
        K.op("act", lambda e: e.activation(out=sq[:, :, 0:ntok], in_=src, func=AF.Square), r=(rsrc,), w=(R_sq,))
        ps, rb = K.bank()
        K.mmg(ps[:, 0:ntok], rb, [(ones_b, sq[:, c, 0:ntok]) for c in range(KC)], r=(R_sq, R_cstb))
        rstd_from(ps[:, 0:ntok], rb, ntok, rstd[:, 0:ntok], R_rstd)
        for c in range(KC):
            K.op("dve", lambda e, c=c: e.scalar_tensor_tensor(out=dst_uT[:, c, :], in0=src[:, c, :], scalar=gcol(l, j, c),
                                                              in1=rstd[:, 0:ntok], op0=ALU.mult, op1=ALU.mult),
                 r=(rsrc, R_rstd, R_gains), w=(R_uT,))

    for tb in range(4):
        xT_block(tb, hblk, R_hblk)
        prenorm_block(hblk, R_hblk, 0, 0, uT[:, :, tb * 512:(tb + 1) * 512])

    if stop == "uT":
        return finish_debug(nc, es, K, A, y_d, uT, R_uT, ident_b, R_cstb, bf=True)

    K.barrier()
    A.top = mark_hT
    qT = A.alloc(BF16, [S])
    kT = A.alloc(BF16, [S])
    R_qT, R_kT = Res("qT"), Res("kT")
    vaug = [A.alloc(BF16, [3, 16, P]) for _ in range(2)]
    R_vaug = [Res("vaug0"), Res("vaug1")]
    acc = A.alloc(F32, [S])
    R_acc = Res("acc")
    PTB = 4
    ptb = [A.alloc(BF16, [384]) for _ in range(PTB)]
    R_pt = [Res(f"pt{i}") for i in range(PTB)]
    pti = [0]
    bias_sb = A.alloc(BF16, [2, 896])
    R_bias = Res("bias")
    K.op("pool", lambda e: e.memset(vaug[0][:, :, :, 64:128], 1.0), w=(R_vaug[0],))
    K.op("pool", lambda e: e.memset(vaug[1][:, :, :, 0:64], 1.0), w=(R_vaug[1],))

    w_in_v = w_in_d.rearrange("(kc p) n -> p kc n", p=P)

    def proj_fm(wap, dst, rdst, rw, scale):
        for tb in range(4):
            ps, rb = K.bank()
            K.mmg(ps, rb, [(wap[:, kc, :], uT[:, kc, tb * 512:(tb + 1) * 512]) for kc in range(KC)], r=(rw, R_uT))
            K.op("act", lambda e, tb=tb: e.activation(out=dst[:, tb * 512:(tb + 1) * 512], in_=ps, func=AF.Copy, scale=scale),
                 x=(rb,), w=(rdst,))

    def tok_slice(d, i):
        L = S // d
        n0 = P * i
        c = n0 // L
        p0 = n0 - c * L
        return slice(c + d * p0, c + d * (p0 + P - 1) + 1, d)

    def proj_v(wv, rw, orders):
        for r_, d in orders:
            for g in range(4):
                ps, rb = K.bank()
                for i4 in range(4):
                    i = g * 4 + i4
                    ts = tok_slice(d, i)
                    K.mmg(ps[:, i4 * P:(i4 + 1) * P], rb, [(uT[:, kc, ts], wv[:, kc, :]) for kc in range(KC)],
                          r=(rw, R_uT), skip=True, first=True)
                psv = ps.rearrange("p (a b) -> p a b", a=4)
                K.op("dve", lambda e, r_=r_, g=g: e.tensor_copy(out=vaug[0][:, r_, g * 4:g * 4 + 4, 0:64], in_=psv[:, :, 0:64]),
                     x=(rb,), w=(R_vaug[0],))
                K.op("act", lambda e, r_=r_, g=g: e.activation(out=vaug[1][:, r_, g * 4:g * 4 + 4, 64:128], in_=psv[:, :, 64:128],
                                                              func=AF.Copy), x=(rb,), w=(R_vaug[1],))

    def banded(hh, r_, d, bias_off, first_branch):
        p0 = 64 * hh
        L = S // d
        QB = min(512, L)
        nqt = QB // P
        ntile = L // P
        its = []
        for c in range(d):
            for q0t in range(0, ntile, nqt):
                kts = [kt for kt in range(q0t - 1, q0t + nqt + 1) if 0 <= kt < ntile]
                for ki, kt in enumerate(kts):
                    its.append((c, q0t, kt, ki == 0, ki == len(kts) - 1))
        state = {}

        def emit_scores(it):
            c, q0t, kt, first, last = it
            qa = max(kt - 1, q0t)
            qb_ = min(kt + 1, q0t + nqt - 1)
            ncol = (qb_ - qa + 1) * P
            boff = bias_off + (qa - kt + 1) * P
            ps, rb = K.bank()
            ksl = slice(c + d * kt * P, c + d * (kt * P + P - 1) + 1, d)
            qsl = slice(c + d * qa * P, c + d * ((qb_ + 1) * P - 1) + 1, d)
            K.mmg(ps[:, 0:ncol], rb,
                  [(ident_b, bias_sb[:, hh, boff:boff + ncol]),
                   (kT[p0:p0 + 64, ksl], qT[p0:p0 + 64, qsl])],
                  r=(R_cstb, R_bias, R_kT, R_qT))
            pi = pti[0] % PTB
            pti[0] += 1
            pt, rpt = ptb[pi], R_pt[pi]
            K.op("act", lambda e: e.activation(out=pt[:, 0:ncol], in_=ps[:, 0:ncol], func=AF.Exp), x=(rb,), w=(rpt,))
            return (pt, rpt, ncol, qa)

        def emit_pv(it, sc):
            c, q0t, kt, first, last = it
            pt, rpt, ncol, qa = sc
            if first:
                state["pv"] = K.bank(hold=True)
            pv, rpv = state["pv"]
            tile_idx = c * ntile + kt
            o0 = (qa - q0t) * P
            K.mmg(pv[:, o0:o0 + ncol], rpv, [(vaug[hh][:, r_, tile_idx, :], pt[:, 0:ncol])],
                  r=(rpt, R_vaug[hh]), skip=True, first=first, last=last)
            if last:
                t0 = c + d * q0t * P
                asl = slice(t0, t0 + d * (QB - 1) + 1, d)
                if first_branch:
                    K.op("dve", lambda e: e.tensor_copy(out=acc[:, asl], in_=pv[:, 0:QB]), x=(rpv,), w=(R_acc,))
                else:
                    K.op("dve", lambda e: e.tensor_tensor(out=acc[:, asl], in0=acc[:, asl], in1=pv[:, 0:QB], op=ALU.add),
                         x=(rpv,), r=(R_acc,), w=(R_acc,))
                K.release(rpv)

        prev = None
        for it in its:
            sc = emit_scores(it)
            if prev is not None:
                emit_pv(*prev)
            prev = (it, sc)
        emit_pv(*prev)

    def normalize(hh, blk, sink_col=None):
        p0 = 64 * hh
        q0 = 64 - p0
        if sink_col is not None:
            K.op("dve", lambda e: e.tensor_scalar(out=acc[q0:q0 + 64, :], in0=acc[q0:q0 + 64, :], scalar1=esink[q0:q0 + 64, sink_col:sink_col + 1],
                                                  scalar2=None, op0=ALU.add), r=(R_acc, R_esink), w=(R_acc,))
        K.op("dve", lambda e: e.reciprocal(out=acc[q0:q0 + 64, :], in_=acc[q0:q0 + 64, :]), r=(R_acc,), w=(R_acc,))
        for tb in range(4):
            ps, rb = K.bank()
            K.mmg(ps, rb, [(swap_f, acc[:, tb * 512:(tb + 1) * 512])], r=(R_cst, R_acc))
            K.op("dve", lambda e, tb=tb, ps=ps: e.tensor_tensor(out=OT[p0:p0 + 64, blk, tb * 512:(tb + 1) * 512],
                                                               in0=acc[p0:p0 + 64, tb * 512:(tb + 1) * 512], in1=ps[p0:p0 + 64, :], op=ALU.mult),
                 x=(rb,), r=(R_acc,), w=(R_OT,))

    for j in range(4):
        wt, rw = getw()
        wv3 = wt[:, 0:KC * 384].rearrange("p (k g n) -> p k g n", k=KC, g=3)
        for g3 in range(3):
            K.dma("pool", wv3[:, :, g3, :], w_in_v[:, :, g3 * 512 + j * P:g3 * 512 + (j + 1) * P], w=(rw,))
        K.dma("pool", bias_sb, biasA_d[2 * j:2 * j + 2].rearrange("h p n -> p h n"), w=(R_bias,))
        proj_fm(wv3[:, :, 0, :], qT, R_qT, rw, 0.125)
        proj_fm(wv3[:, :, 1, :], kT, R_kT, rw, 1.0)
        proj_v(wv3[:, :, 2, :], rw, [(0, 1), (1, 4), (2, 16)])
        for hh in range(2):
            banded(hh, 0, 1, 0, True)
            banded(hh, 1, 4, 384, False)
            banded(hh, 2, 16, 768 - P, False)
            normalize(hh, j)

    if stop == "attnA":
        return finish_debug(nc, es, K, A, y_d, OT, R_OT, ident_b, R_cstb, bf=True)

    wt, rw = getw()
    wkv = wt[:, 0:KC * 256].rearrange("p (k g n) -> p k g n", k=KC, g=2)
    for g2 in range(2):
        K.dma("pool", wkv[:, :, g2, :], w_in_v[:, :, 2048 + g2 * P:2048 + (g2 + 1) * P], w=(rw,))
    proj_fm(wkv[:, :, 0, :], kT, R_kT, rw, 1.0)
    proj_v(wkv[:, :, 1, :], rw, [(0, 1)])
    for jb in range(4):
        wt, rw = getw()
        wq = wt[:, 0:KC * P].rearrange("p (k g n) -> p k g n", k=KC, g=2)
        for g2 in range(2):
            c0 = 1536 + (jb + 4 * g2) * 64
            K.dma("pool", wq[:, :, g2, :], w_in_v[:, :, c0:c0 + 64], w=(rw,))
        K.dma("pool", bias_sb[:, :, 0:384], biasB_d[jb:jb + 5:4].rearrange("h p n -> p h n"), w=(R_bias,))
        proj_fm(wt[:, 0:KC * P].rearrange("p (k n) -> p k n", k=KC), qT, R_qT, rw, 0.125)
        for hh in range(2):
            banded(hh, 0, 1, 0, True)
            normalize(hh, 4 + jb, sink_col=jb + 4 * hh)

    if stop == "attn":
        return finish_debug(nc, es, K, A, y_d, OT, R_OT, ident_b, R_cstb, bf=True)

    K.barrier()
    A.top = mark_c
    usb = A.alloc(F32, [KC, 512])
    R_usb = Res("usb")
    wo = wall[:, 0:KC * D].rearrange("p (k n) -> p k n", k=KC)
    R_wo = R_w[0]
    wi[0] = 2
    K.dma("pool", wo[:, 0:4, :], w_out_d[0:512, :].rearrange("(b p) n -> p b n", p=P), w=(R_wo,))
    wob = w_out_d[512:1024, :].rearrange("(g jb q) n -> q g jb n", g=2, jb=4)
    K.dma("pool", wo[0:64, 4:8, :], wob[:, 0], w=(R_wo,))
    K.dma("pool", wo[64:128, 4:8, :], wob[:, 1], w=(R_wo,))

    def postnorm_add(l, j, tb, ntok, src, rsrc, base_fn):
        K.op("act", lambda e: e.activation(out=sq[:, :, 0:ntok], in_=src, func=AF.Square), r=(rsrc,), w=(R_sq,))
        ps, rb = K.bank()
        K.mmg(ps[:, 0:ntok], rb, [(ones_b, sq[:, c, 0:ntok]) for c in range(KC)], r=(R_sq, R_cstb))
        rstd_from(ps[:, 0:ntok], rb, ntok, rstd[:, 0:ntok], R_rstd)
        if stop == "pn":
            for c in range(KC):
                K.op("dve", lambda e, c=c: e.scalar_tensor_tensor(out=hT[:, c, tb * 512:(tb + 1) * 512], in0=src[:, c, :], scalar=gcol(l, j, c),
                                                                  in1=rstd[:, 0:ntok], op0=ALU.mult, op1=ALU.mult),
                     r=(rsrc, R_rstd, R_gains), w=(R_hT,))
            return
        for c in range(KC):
            K.op("dve", lambda e, c=c: e.scalar_tensor_tensor(out=src[:, c, :], in0=src[:, c, :], scalar=gcol(l, j, c),
                                                              in1=rstd[:, 0:ntok], op0=ALU.mult, op1=ALU.mult),
                 r=(rsrc, R_rstd, R_gains), w=(rsrc,))
        base_fn(src, rsrc)

    def outproj_block(tb, wmat, rwm, inT, rin, nblk):
        for m in range(KC):
            ps, rb = K.bank()
            K.mmg(ps, rb, [(wmat[:, b, m * P:(m + 1) * P], inT[:, b, tb * 512:(tb + 1) * 512]) for b in range(nblk)], r=(rwm, rin))
            K.op("act", lambda e, m=m, ps=ps: e.activation(out=usb[:, m, :], in_=ps, func=AF.Copy), x=(rb,), w=(R_usb,))

    for tb in range(4):
        outproj_block(tb, wo, R_wo, OT, R_OT, 8)
        if stop == "um":
            K.op("dve", lambda e: e.tensor_copy(out=hT[:, :, tb * 512:(tb + 1) * 512], in_=usb), r=(R_usb,), w=(R_hT,))
            continue

        def base_x(src, rsrc, tb=tb):
            for t4 in range(4):
                tt = tb * 4 + t4
                xt, rx = load_x_tile(tt)
                for half in range(2):
                    ps, rb = K.bank()
                    for c4 in range(4):
                        c = half * 4 + c4
                        K.op("pe", lambda e, c=c, c4=c4: e.transpose(ps[:, c4 * P:(c4 + 1) * P], xt[:, c * P:(c + 1) * P], ident_f),
                             r=(rx, R_cst), w=(rb,), inc=(c4 == 3))
                    K.op("dve", lambda e, half=half, t4=t4, ps=ps: e.tensor_tensor(
                        out=hT[:, half * 4:half * 4 + 4, tb * 512 + t4 * P:tb * 512 + (t4 + 1) * P],
                        in0=src[:, half * 4:half * 4 + 4, t4 * P:(t4 + 1) * P],
                        in1=ps.rearrange("p (a b) -> p a b", a=4), op=ALU.add),
                        x=(rb,), r=(rsrc,), w=(R_hT,))
        if stop == "pn":
            def base_dbg(src, rsrc, tb=tb):
                K.op("dve", lambda e: e.tensor_copy(out=hT[:, :, tb * 512:(tb + 1) * 512], in_=src), r=(rsrc,), w=(R_hT,))
            postnorm_add(0, 1, tb, 512, usb, R_usb, base_dbg)
        else:
            postnorm_add(0, 1, tb, 512, usb, R_usb, base_x)

    if stop in ("l0attn", "um", "pn"):
        return finish_debug(nc, es, K, A, y_d, hT, R_hT, ident_f, R_cst, bf=False)

    R_wx = [Res("wx0"), Res("wx1")]
    w6i = [0]

    def mlp(l):
        K.barrier()
        A.top = mark_top
        acc2 = A.alloc(F32, [KC, 1024])
        R_acc2 = Res("acc2")
        h1 = [A.alloc(BF16, [4, 1024]) for _ in range(2)]
        R_h1 = [Res("h1a"), Res("h1b")]
        rl = [A.alloc(BF16, [512]) for _ in range(2)]
        R_rl = [Res("rl0"), Res("rl1")]
        w1v = w1_d[l].rearrange("(kc p) n -> p kc n", p=P)
        uflat = uT.rearrange("p k n -> p (k n)")
        uTm = uflat[:, 0:KC * 1024].rearrange("p (k n) -> p k n", k=KC)
        slots6 = wslot + [uflat[:, 8192:12288], uflat[:, 12288:16384]]
        R_slots6 = R_w + R_wx

        def getw6():
            i = w6i[0] % 6
            w6i[0] += 1
            return slots6[i], R_slots6[i]
        w2v = w2_d[l].rearrange("(g kc p) n -> g p kc n", kc=4, p=P)
        rli = 0
        for th in range(2):
            t0 = th * 1024
            for ts in range(2):
                prenorm_block(hT[:, :, t0 + ts * 512:t0 + (ts + 1) * 512], R_hT, l, 2, uTm[:, :, ts * 512:(ts + 1) * 512])
            def loadw(g):
                wa, rwa = getw6()
                wb, rwb = getw6()
                w1g = wa.rearrange("p (k n) -> p k n", k=KC)
                w2g = wb.rearrange("p (k n) -> p k n", k=4)
                K.dma("pool", w1g, w1v[:, :, g * 512:(g + 1) * 512], w=(rwa,))
                K.dma("pool", w2g, w2v[g], w=(rwb,))
                return w1g, rwa, w2g, rwb
            pre = [loadw(0), loadw(1)]
            for g in range(8):
                w1g, rwa, w2g, rwb = pre.pop(0)
                if g + 2 < 8:
                    pre.append(loadw(g + 2))
                hb, rhb = h1[g % 2], R_h1[g % 2]
                for fb in range(4):
                    for ts in range(2):
                        ps, rb = K.bank()
                        K.mmg(ps, rb, [(w1g[:, kc, fb * P:(fb + 1) * P], uTm[:, kc, ts * 512:(ts + 1) * 512]) for kc in range(KC)],
                              r=(rwa, R_uT))
                        rt, rrt = rl[rli % 2], R_rl[rli % 2]
                        rli += 1
                        K.op("act", lambda e, rt=rt, ps=ps: e.activation(out=rt, in_=ps, func=AF.Relu), x=(rb,), w=(rrt,))
                        K.op("pool", lambda e, rt=rt, fb=fb, ts=ts, hb=hb: e.tensor_tensor(out=hb[:, fb, ts * 512:(ts + 1) * 512], in0=rt, in1=rt, op=ALU.mult),
                             r=(rrt,), w=(rhb,))
                for m in range(KC):
                    for ts in range(2):
                        ps, rb = K.bank()
                        K.mmg(ps, rb, [(w2g[:, fb, m * P:(m + 1) * P], hb[:, fb, ts * 512:(ts + 1) * 512]) for fb in range(4)],
                              r=(rwb, rhb))
                        dst = acc2[:, m, ts * 512:(ts + 1) * 512]
                        if g == 0:
                            K.op("act", lambda e, dst=dst, ps=ps: e.activation(out=dst, in_=ps, func=AF.Copy), x=(rb,), w=(R_acc2,))
                        else:
                            K.op("dve", lambda e, dst=dst, ps=ps: e.tensor_tensor(out=dst, in0=dst, in1=ps, op=ALU.add),
                                 x=(rb,), r=(R_acc2,), w=(R_acc2,))
            for ts in range(2):
                tsl = slice(t0 + ts * 512, t0 + (ts + 1) * 512)

                def base_h(src, rsrc, tsl=tsl):
                    K.op("pool", lambda e: e.tensor_tensor(out=hT[:, :, tsl], in0=hT[:, :, tsl], in1=src, op=ALU.add),
                         r=(rsrc, R_hT), w=(R_hT,))
                postnorm_add(l, 3, None, 512, acc2[:, :, ts * 512:(ts + 1) * 512], R_acc2, base_h)

    mlp(0)
    if stop == "l0":
        return finish_debug(nc, es, K, A, y_d, hT, R_hT, ident_f, R_cst, bf=False)


    def rwkv():
        E05 = math.exp(-0.5)
        import os
        skipc = {}
        def hit(name):
            if stop != name:
                return False
            skipc[name] = skipc.get(name, 0) + 1
            return skipc[name] > int(os.environ.get("RK_SKIP", "0"))
        K.barrier()
        for tb in range(4):
            prenorm_block(hT[:, :, tb * 512:(tb + 1) * 512], R_hT, 1, 0, uT[:, :, tb * 512:(tb + 1) * 512])
        K.barrier()
        yT = OT
        R_yT = R_OT
        A.top = X0
        hw = A.alloc(BF16, [S]); ha = A.alloc(BF16, [S]); hg = A.alloc(BF16, [2, S])
        R_hw, R_ha, R_hg = Res("hw"), Res("ha"), Res("hg")
        coef = A.alloc(F32, [6, 3, 8]); rkv = A.alloc(F32, [72]); mu = A.alloc(F32, [96]); gC = A.alloc(F32, [4])
        R_coef, R_rkv, R_mu, R_gC = Res("coef"), Res("rkv"), Res("mu"), Res("gC")
        assert A.top <= X1, (A.top, X1)
        A.top = W0
        chunkbase = A.top
        stg = A.alloc(F32, [KC, P]); R_stg = Res("stg")
        wsl = [A.alloc(BF16, [3, KC, P]) for _ in range(2)]; R_wsl = [Res("ws0"), Res("ws1")]
        wsend = A.top
        wsi = [0]
        A.top = chunkbase
        MRc = [A.alloc(BF16, [2, 256]) for _ in range(4)]; R_MRc = [Res(f"MR{i}") for i in range(4)]
        KRc = [A.alloc(BF16, [2, 256]) for _ in range(4)]; R_KRc = [Res(f"KR{i}") for i in range(4)]
        MPc = [A.alloc(BF16, [2, 256]) for _ in range(4)]; R_MPc = [Res(f"MP{i}") for i in range(4)]
        NNc = [A.alloc(BF16, [2, P]) for _ in range(4)]; R_NNc = [Res(f"NN{i}") for i in range(4)]
        Wbc = [A.alloc(BF16, [2, 64]) for _ in range(4)]; R_Wbc = [Res(f"Wb{i}") for i in range(4)]
        Ubc = [A.alloc(BF16, [2, 64]) for _ in range(4)]; R_Ubc = [Res(f"Ub{i}") for i in range(4)]
        A.top = max(A.top, wsend)
        l2w = A.alloc(BF16, [4, P]); R_l2w = Res("l2w")
        T3 = A.alloc(F32, [512]); T4 = A.alloc(F32, [512]); R_T3, R_T4 = Res("T3"), Res("T4")
        AR = A.alloc(BF16, [4, 2, P]); R_AR = Res("AR")
        BKf = A.alloc(BF16, [2, 512]); R_BKf = Res("BKf")
        kd = A.alloc(BF16, [512]); R_kd = Res("kd")
        BKt = A.alloc(BF16, [4, 2, P]); R_BKt = Res("BKt")
        Sf = A.alloc(F32, [64]); Sb = A.alloc(BF16, [64]); R_Sf, R_Sb = Res("Sf"), Res("Sb")
        cst2 = A.alloc(BF16, [11 * P]); R_cst2 = Res("cst2")
        assert A.top <= W1, (A.top, W1)
        A.top = mark_c
        rT = A.alloc(BF16, [S]); kT2 = A.alloc(BF16, [S]); vT = A.alloc(BF16, [S]); vtm = A.alloc(BF16, [16, P]); kkT = A.alloc(BF16, [S])
        R_rT, R_kT2, R_vT, R_vtm, R_kkT = Res("rT"), Res("kT2"), Res("vT"), Res("vtm"), Res("kkT")
        T1 = A.alloc(F32, [4, 129]); T2 = A.alloc(F32, [4, 129]); R_T1, R_T2 = Res("T1"), Res("T2")

        K.dma("pool", cst2, cst2_d, w=(R_cst2,))
        K.dma("sp", rkv, rkv_d, w=(R_rkv,))
        K.dma("sp", mu, mu_d, w=(R_mu,))
        LT2 = cst2[:, 0:2 * P]
        GT2 = cst2[:, 2 * P:4 * P]
        LTm = cst2[:, 0:P]
        GTm = cst2[:, 2 * P:3 * P]
        bsum = cst2[:, 4 * P:5 * P]
        bmean_f = cst[:, 3 * P:4 * P]
        ones_f = cst[:, 2 * P:3 * P]
        BD16 = cst2[:, 5 * P:6 * P]
        OFFS = cst2[:, 6 * P:9 * P].rearrange("p (a b) -> p a b", a=3)
        NDf = cst2[:, 9 * P:10 * P]
        NDr = cst2[:, 10 * P:11 * P]
        def mo_view(t):
            return t.bitcast(BF16)[:, 0:768].rearrange("p (a b c) -> p a b c", a=3, b=2)
        T1fl = T1.rearrange("p a b -> p (a b)")
        T2fl = T2.rearrange("p a b -> p (a b)")
        MOc = [mo_view(T3), mo_view(T4), mo_view(T1fl), mo_view(T2fl)]
        R_MOc = [R_T3, R_T4, R_T1, R_T2]
        muv = mu.rearrange("p (s t c) -> p s t c", s=6, t=2)
        K.op("dve", lambda e: e.tensor_copy(out=coef[:, :, 1:3, :], in_=muv), r=(R_mu,), w=(R_coef,))
        K.op("dve", lambda e: e.tensor_tensor(out=coef[:, :, 0, :], in0=muv[:, :, 0, :], in1=muv[:, :, 1, :], op=ALU.add), r=(R_mu,), w=(R_coef,))
        K.op("dve", lambda e: e.tensor_scalar(out=coef[:, :, 0, :], in0=coef[:, :, 0, :], scalar1=-1.0, scalar2=1.0, op0=ALU.mult, op1=ALU.add),
             r=(R_coef,), w=(R_coef,))

        def rcol(i, b):
            return rkv[:, i * 8 + b:i * 8 + b + 1]

        def load_scaled(wsrc, stream):
            K.dma("sp", stg, wsrc.rearrange("(kc p) n -> p kc n", p=P), w=(R_stg,))
            wsi[0] += 1
            ws, R_ws = wsl[wsi[0] % 2], R_wsl[wsi[0] % 2]
            for t in range(3):
                K.op("pool", lambda e, t=t: e.tensor_tensor(out=ws[:, t], in0=stg, in1=coef[:, stream, t, :].unsqueeze(2).to_broadcast([P, KC, P]), op=ALU.mult),
                     r=(R_stg, R_coef), w=(R_ws,))

        def proj3(evac):
            ws, R_ws = wsl[wsi[0] % 2], R_wsl[wsi[0] % 2]
            for tb in range(4):
                t0 = tb * 512
                ps, rb = K.bank()
                terms = [(ws[:, 0, kc, :], uT[:, kc, t0:t0 + 512], ps) for kc in range(KC)]
                if tb == 0:
                    terms += [(ws[:, 1, kc, :], uT[:, kc, 0:511], ps[:, 1:512]) for kc in range(KC)]
                else:
                    terms += [(ws[:, 1, kc, :], uT[:, kc, t0 - 1:t0 + 511], ps) for kc in range(KC)]
                if tb == 3:
                    terms += [(ws[:, 2, kc, :], uT[:, kc, t0 + 1:S], ps[:, 0:511]) for kc in range(KC)]
                else:
                    terms += [(ws[:, 2, kc, :], uT[:, kc, t0 + 1:t0 + 513], ps) for kc in range(KC)]
                import os as _os
                terms = terms[:int(_os.environ.get("RK_NT", "24"))]
                n = len(terms)
                for i, (l, rh, o) in enumerate(terms):
                    K.op("pe", lambda e, l=l, rh=rh, o=o, i=i: e.matmul(o, lhsT=l, rhs=rh, start=(i == 0), stop=(i == n - 1), skip_group_check=True),
                         r=(R_ws, R_uT), w=(rb,), inc=(i == n - 1))
                evac(tb, ps, rb)

        if stop == "rk_c":
            return "dbg_y"
        load_scaled(lora1_d[:, 0:128], 1)
        if stop == "rk_ls":
            return "dbg_y"
        proj3(lambda tb, ps, rb: K.op("act", lambda e: e.activation(out=hw[:, tb * 512:(tb + 1) * 512], in_=ps, func=AF.Tanh), x=(rb,), w=(R_hw,)))
        load_scaled(lora1_d[:, 128:256], 4)
        proj3(lambda tb, ps, rb: K.op("act", lambda e: e.activation(out=ha[:, tb * 512:(tb + 1) * 512], in_=ps, func=AF.Copy), x=(rb,), w=(R_ha,)))
        for d in range(2):
            load_scaled(lora1_d[:, 256 + d * 128:256 + (d + 1) * 128], 5)
            proj3(lambda tb, ps, rb, d=d: K.op("act", lambda e: e.activation(out=hg[:, d, tb * 512:(tb + 1) * 512], in_=ps, func=AF.Sigmoid), x=(rb,), w=(R_hg,)))

        if stop == "rk_pre0":
            return "dbg_y"
        if stop == "rk_pre":
            K.op("dve", lambda e: e.tensor_copy(out=OT[:, 0, :], in_=hw), r=(R_hw,), w=(R_OT,))
            K.op("dve", lambda e: e.tensor_copy(out=OT[:, 1, :], in_=ha), r=(R_ha,), w=(R_OT,))
            K.op("dve", lambda e: e.tensor_copy(out=OT[:, 2:4, :], in_=hg), r=(R_hg,), w=(R_OT,))
            return "dbg_y"
        NBLK = int(os.environ.get("RK_NB", "8"))
        NDIR = int(os.environ.get("RK_ND", "2"))
        for b in range(NBLK):
            bc = slice(b * P, (b + 1) * P)
            K.barrier()
            K.dma("pool", l2w[:, 0, :], w2c_d[:, bc], w=(R_l2w,))
            K.dma("pool", l2w[:, 1, :], a2c_d[:, bc], w=(R_l2w,))
            K.dma("pool", l2w[:, 2:4, :], g2_d[:, :, bc].rearrange("d p n -> p d n"), w=(R_l2w,))
            load_scaled(wr_d[:, bc], 0)
            proj3(lambda tb, ps, rb: K.op("act", lambda e: e.activation(out=rT[:, tb * 512:(tb + 1) * 512], in_=ps, func=AF.Copy), x=(rb,), w=(R_rT,)))
            load_scaled(wk_d[:, bc], 2)
            proj3(lambda tb, ps, rb: K.op("act", lambda e: e.activation(out=kT2[:, tb * 512:(tb + 1) * 512], in_=ps, func=AF.Copy), x=(rb,), w=(R_kT2,)))
            load_scaled(wv_d[:, bc], 3)
            proj3(lambda tb, ps, rb: K.op("act", lambda e: e.activation(out=vT[:, tb * 512:(tb + 1) * 512], in_=ps, func=AF.Copy), x=(rb,), w=(R_vT,)))
            for g4 in range(4):
                ps, rb = K.bank()
                for i4 in range(4):
                    i = g4 * 4 + i4
                    K.op("pe", lambda e, i=i, i4=i4: e.matmul(ps[:, i4 * P:(i4 + 1) * P], lhsT=vT[:, i * P:(i + 1) * P], rhs=ident_b, start=True, stop=True,
                                                            skip_group_check=True), r=(R_vT, R_cstb), w=(rb,), inc=(i4 == 3))
                K.op("act", lambda e, g4=g4: e.activation(out=vtm[:, g4 * 4:g4 * 4 + 4, :], in_=ps.rearrange("p (a b) -> p a b", a=4), func=AF.Copy),
                     x=(rb,), w=(R_vtm,))
            for tb in range(4):
                tsl = slice(tb * 512, (tb + 1) * 512)
                K.op("act", lambda e: e.activation(out=kd, in_=kT2[:, tsl], func=AF.Square, scale=rcol(0, b)), r=(R_kT2, R_rkv), w=(R_kd,))
                ps, rb = K.bank()
                K.mmg(ps, rb, [(bsum, kd)], r=(R_kd, R_cst2))
                K.op("act", lambda e: e.activation(out=T3, in_=ps, func=AF.Ln, bias=epsc[:, 2:3]), x=(rb,), w=(R_T3,), r=(R_epsc,))
                K.op("act", lambda e: e.activation(out=T3, in_=T3, func=AF.Exp, scale=-0.5), r=(R_T3,), w=(R_T3,))
                K.op("dve", lambda e: e.scalar_tensor_tensor(out=kkT[:, tsl], in0=kT2[:, tsl], scalar=rcol(0, b), in1=T3, op0=ALU.mult, op1=ALU.mult),
                     r=(R_kT2, R_T3, R_rkv), w=(R_kkT,))

            K.barrier()
            if stop == "rk_proj":
                K.op("dve", lambda e: e.tensor_copy(out=OT[:, 0, :], in_=rT), r=(R_rT,), w=(R_OT,))
                K.op("dve", lambda e: e.tensor_copy(out=OT[:, 1, :], in_=kT2), r=(R_kT2,), w=(R_OT,))
                K.op("dve", lambda e: e.tensor_copy(out=OT[:, 2, :], in_=vT), r=(R_vT,), w=(R_OT,))
                K.op("dve", lambda e: e.tensor_copy(out=OT[:, 3, :], in_=kkT), r=(R_kkT,), w=(R_OT,))
                return "dbg_y"
            for d in range(int(os.environ.get("RK_D0", "0")), NDIR):
                fwd = d == 0
                q64 = slice(64 * d, 64 * d + 64)
                mask2 = LT2 if fwd else GT2
                nmask = GTm if fwd else LTm
                K.op("dve", lambda e: e.memset(Sf, 0.0), w=(R_Sf,))
                for tb in list(range(4) if fwd else range(3, -1, -1))[:int(os.environ.get("RK_NTB", "4"))]:
                    tsl = slice(tb * 512, (tb + 1) * 512)
                    ps, rb = K.bank()
                    K.mmg(ps, rb, [(l2w[q64, 0, :], hw[q64, tsl])], r=(R_l2w, R_hw))
                    K.op("act", lambda e: e.activation(out=T3, in_=ps, func=AF.Sigmoid, bias=rcol(5 + d, b)), x=(rb,), w=(R_T3,), r=(R_rkv,))
                    K.op("act", lambda e: e.activation(out=T3, in_=T3, func=AF.Exp, scale=-E05), r=(R_T3,), w=(R_T3,))
                    ps, rb = K.bank()
                    K.mmg(ps, rb, [(l2w[q64, 1, :], ha[q64, tsl])], r=(R_l2w, R_ha))
                    K.op("act", lambda e: e.activation(out=T4, in_=ps, func=AF.Sigmoid, bias=rcol(7 + d, b)), x=(rb,), w=(R_T4,), r=(R_rkv,))
                    K.op("dve", lambda e: e.memset(T2[:, :, 0:1], 1.0), w=(R_T2,))
                    for ch in range(4):
                        K.op("dve", lambda e, ch=ch: e.tensor_tensor_scan(out=T2[:, ch, 1:129], data0=T3[:, ch * P:(ch + 1) * P], data1=ones_f,
                                                                         initial=1.0, op0=ALU.mult, op1=ALU.mult),
                             r=(R_T3, R_cst), w=(R_T2,))
                    K.op("dve", lambda e: e.tensor_copy(out=gC, in_=T2[:, :, 128]), r=(R_T2,), w=(R_gC,))
                    K.op("dve", lambda e: e.reciprocal(out=T1, in_=T2), r=(R_T2,), w=(R_T1,))
                    if fwd:
                        K.op("dve", lambda e: e.tensor_copy(out=T2[:, :, 0:1], in_=T1[:, :, 128:129]), r=(R_T1,), w=(R_T2,))
                        for ch in range(4):
                            K.op("dve", lambda e, ch=ch: e.tensor_scalar(out=T1[:, ch, 1:129], in0=T1[:, ch, 1:129], scalar1=gC[:, ch:ch + 1], scalar2=None,
                                                                        op0=ALU.mult), r=(R_T1, R_gC), w=(R_T1,))
                        K.op("dve", lambda e: e.reciprocal(out=T2[:, :, 1:129], in_=T1[:, :, 1:129]), r=(R_T1,), w=(R_T2,))
                        PA, PR, PB = T2[:, :, 0:128], T2[:, :, 1:129], T1[:, :, 1:129]
                    else:
                        PA, PR, PB = T1[:, :, 1:129], T1[:, :, 0:128], T2[:, :, 0:128]
                    v4 = lambda ap: ap.rearrange("p (a b) -> p a b", a=4)
                    K.op("dve", lambda e: e.tensor_scalar(out=T3, in0=T4, scalar1=-1.0, scalar2=rcol(1, b), op0=ALU.add, op1=ALU.mult),
                         r=(R_T4, R_rkv), w=(R_T3,))
                    K.op("dve", lambda e: e.scalar_tensor_tensor(out=kd, in0=T3, scalar=1.0, in1=kT2[:, tsl], op0=ALU.add, op1=ALU.mult),
                         r=(R_T3, R_kT2), w=(R_kd,))
                    K.op("dve", lambda e: e.tensor_tensor(out=T3, in0=T4, in1=kkT[:, tsl], op=ALU.mult), r=(R_T4, R_kkT), w=(R_T3,))
                    K.op("dve", lambda e: e.tensor_tensor(out=v4(BKf[:, 0, :]), in0=v4(T3), in1=PB, op=ALU.mult), r=(R_T3, R_T1, R_T2), w=(R_BKf,))
                    K.op("dve", lambda e: e.tensor_tensor(out=v4(BKf[:, 1, :]), in0=v4(kd), in1=PB, op=ALU.mult), r=(R_kd, R_T1, R_T2), w=(R_BKf,))
                    K.op("dve", lambda e: e.scalar_tensor_tensor(out=AR[:, :, 0, :], in0=v4(kkT[:, tsl]), scalar=-1.0, in1=PA, op0=ALU.mult, op1=ALU.mult),
                         r=(R_kkT, R_T1, R_T2), w=(R_AR,))
                    K.op("dve", lambda e: e.tensor_tensor(out=AR[:, :, 1, :], in0=v4(rT[:, tsl]), in1=PR, op=ALU.mult), r=(R_rT, R_T1, R_T2), w=(R_AR,))
                    if hit("rk_d1"):
                        K.op("dve", lambda e: e.tensor_copy(out=OT[:, 0, 0:1024], in_=AR.rearrange("p a b c -> p (a b c)")), r=(R_AR,), w=(R_OT,))
                        K.op("dve", lambda e: e.tensor_copy(out=OT[:, 1, 0:1024], in_=BKf.rearrange("p a b -> p (a b)")), r=(R_BKf,), w=(R_OT,))
                        K.op("dve", lambda e: e.tensor_copy(out=OT[:, 2, 0:512], in_=kd), r=(R_kd,), w=(R_OT,))
                        return "dbg_y"
                    for half in range(2):
                        ps, rb = K.bank()
                        for c2 in range(2):
                            ch = half * 2 + c2
                            for w_ in range(2):
                                K.op("pe", lambda e, ch=ch, c2=c2, w_=w_: e.matmul(ps[:, (c2 * 2 + w_) * P:(c2 * 2 + w_ + 1) * P], lhsT=BKf[:, w_, ch * P:(ch + 1) * P],
                                                                                 rhs=ident_b, start=True, stop=True, skip_group_check=True),
                                     r=(R_BKf, R_cstb), w=(rb,), inc=(c2 == 1 and w_ == 1))
                        K.op("act", lambda e, half=half: e.activation(out=BKt[:, half * 2:half * 2 + 2, :, :], in_=ps.rearrange("p (a b c) -> p a b c", a=2, b=2),
                                                                      func=AF.Copy), x=(rb,), w=(R_BKt,))
                    ops_, rops = K.bank(hold=True)
                    ops2, rops2 = K.bank(hold=True)
                    v2 = lambda ap: ap.rearrange("p (a b) -> p a b", a=2)

                    def chunk_gen(ch):
                        csl = slice(ch * P, (ch + 1) * P)
                        gtile = tb * 4 + ch
                        MR, KR, MPb, NNb, Wb, Ub, MO = MRc[ch], KRc[ch], MPc[ch], NNc[ch], Wbc[ch], Ubc[ch], MOc[ch]
                        R_MR, R_KR, R_MP, R_NN, R_Wb, R_Ub, R_MO = R_MRc[ch], R_KRc[ch], R_MPc[ch], R_NNc[ch], R_Wbc[ch], R_Ubc[ch], R_MOc[ch]
                        Mh = MPb[:, :, 0:128]
                        Qh = MPb[:, :, 128:256]
                        bX, rX = K.bank(); bY, rY = K.bank(); bZ, rZ = K.bank()
                        Bh = lambda h: BKf[64 * h:64 * h + 64, 0, csl]
                        Kh = lambda h: BKf[64 * h:64 * h + 64, 1, csl]
                        ARh = lambda h: AR[64 * h:64 * h + 64, ch, :, :]
                        Ah = lambda h: AR[64 * h:64 * h + 64, ch, 0, :]
                        seq = [(bX, rX, 0, Bh(0), ARh(0), 256), (bY, rY, 1, Kh(1), ARh(1), 256), (bZ, rZ, 0, Ah(0), Bh(0), 128),
                               (bX, rX, 1, Bh(1), ARh(1), 256), (bY, rY, 0, Kh(0), ARh(0), 256), (bZ, rZ, 1, Ah(1), Bh(1), 128)]
                        for si, (bk, rk_, h, l, rh, wdt) in enumerate(seq):
                            K.op("pe", lambda e, bk=bk, h=h, l=l, rh=rh, wdt=wdt: e.matmul(bk[:, h * wdt:(h + 1) * wdt], lhsT=l, rhs=rh, start=True, stop=True,
                                                                                         skip_group_check=True),
                                 r=(R_BKf, R_AR), w=(rk_,), inc=(si >= 3))
                        m2b = mask2.unsqueeze(1).to_broadcast([P, 2, 256])
                        K.op("dve", lambda e: e.tensor_tensor(out=MR, in0=v2(bX), in1=m2b, op=ALU.mult), x=(rX,), r=(R_cst2,), w=(R_MR,))
                        K.op("act", lambda e: e.activation(out=KR, in_=v2(bY), func=AF.Copy), x=(rY,), w=(R_KR,))
                        K.op("pool", lambda e: e.tensor_tensor(out=KR, in0=KR, in1=m2b, op=ALU.mult), r=(R_KR, R_cst2), w=(R_KR,))
                        ndm = NDf if fwd else NDr
                        K.op("dve", lambda e: e.tensor_tensor(out=NNb, in0=v2(bZ[:, 0:256]), in1=ndm.unsqueeze(1).to_broadcast([P, 2, P]), op=ALU.mult),
                             x=(rZ,), r=(R_cst2,), w=(R_NN,))
                        yield
                        Mfull = MR[:, :, 0:128]
                        K.op("pool", lambda e: e.tensor_tensor(out=Mh, in0=Mfull, in1=BD16.unsqueeze(1).to_broadcast([P, 2, P]), op=ALU.mult),
                             r=(R_MR, R_cst2), w=(R_MP,))
                        K.op("pool", lambda e: e.tensor_tensor(out=Qh, in0=Mh, in1=ident_b.unsqueeze(1).to_broadcast([P, 2, P]), op=ALU.add),
                             r=(R_MP, R_cstb), w=(R_MP,))
                        K.op("pool", lambda e: e.tensor_tensor(out=MO, in0=Mfull.unsqueeze(1).to_broadcast([P, 3, 2, P]),
                                                               in1=OFFS.unsqueeze(2).to_broadcast([P, 3, 2, P]), op=ALU.mult),
                             r=(R_MR, R_cst2), w=(R_MO,))
                        yield
                        for k in range(0, 4):
                            last = k == 3
                            bk, rk_ = K.bank()
                            for h in range(2):
                                if k == 0:
                                    K.op("pe", lambda e, h=h: e.matmul(bk[:, h * 256:h * 256 + 128], lhsT=NNb[:, h, :], rhs=MPb[:, h, 0:128],
                                                                       start=True, stop=True, skip_group_check=True), r=(R_NN, R_MP), w=(rk_,), inc=(h == 1))
                                elif last:
                                    K.op("pe", lambda e, h=h: e.matmul(bk[:, h * 256 + 128:h * 256 + 256], lhsT=NNb[:, h, :], rhs=MPb[:, h, 128:256],
                                                                       start=True, stop=True, skip_group_check=True), r=(R_NN, R_MP), w=(rk_,), inc=(h == 1))
                                else:
                                    K.op("pe", lambda e, h=h: e.matmul(bk[:, h * 256:(h + 1) * 256], lhsT=NNb[:, h, :], rhs=MPb[:, h, :],
                                                                       start=True, stop=True, skip_group_check=True), r=(R_NN, R_MP), w=(rk_,), inc=(h == 1))
                            if not last:
                                bk2, rk2 = K.bank()
                                for h in range(2):
                                    K.op("pe", lambda e, h=h: e.matmul(bk2[:, h * P:(h + 1) * P], lhsT=MPb[:, h, 0:128], rhs=NNb[:, h, :],
                                                                       start=True, stop=True, skip_group_check=True), r=(R_NN, R_MP), w=(rk2,), inc=(h == 1))
                            bkv = v2(bk)
                            if k > 0:
                                K.op("dve", lambda e: e.tensor_tensor(out=Qh, in0=Qh, in1=bkv[:, :, 128:256], op=ALU.add), x=(rk_,), r=(R_MP,), w=(R_MP,))
                            if not last:
                                K.op("act", lambda e: e.activation(out=Mh, in_=bkv[:, :, 0:128], func=AF.Copy), x=(rk_,), w=(R_MP,))
                                K.op("dve", lambda e: e.tensor_copy(out=NNb, in_=v2(bk2[:, 0:256])), x=(rk2,), w=(R_NN,))
                            yield
                        bk, rk_ = K.bank()
                        for h in range(2):
                            K.op("pe", lambda e, h=h: e.matmul(bk[:, h * P:(h + 1) * P], lhsT=MPb[:, h, 128:256], rhs=ident_b, start=True, stop=True, skip_group_check=True),
                                 r=(R_MP, R_cstb), w=(rk_,), inc=(h == 1))
                        K.op("act", lambda e: e.activation(out=NNb, in_=v2(bk[:, 0:256]), func=AF.Copy), x=(rk_,), w=(R_NN,))
                        yield
                        for ni in range(3):
                            lastm = ni == 2
                            bk, rk_ = K.bank()
                            for h in range(2):
                                K.op("pe", lambda e, h=h: e.matmul(bk[:, h * P:(h + 1) * P], lhsT=MO[:, ni, h, :], rhs=NNb[:, h, :], start=True, stop=True, skip_group_check=True),
                                     r=(R_MO, R_NN), w=(rk_,), inc=(h == 1))
                            K.op("act", lambda e: e.activation(out=Mh, in_=v2(bk[:, 0:256]), func=AF.Copy), x=(rk_,), w=(R_MP,))
                            yield
                            bk, rk_ = K.bank()
                            for h in range(2):
                                K.op("pe", lambda e, h=h: e.matmul(bk[:, h * P:(h + 1) * P], lhsT=MPb[:, h, 0:128], rhs=MPb[:, h, 128:256], start=True, stop=True,
                                                                   skip_group_check=True), r=(R_MP,), w=(rk_,), inc=(lastm and h == 1))
                            if not lastm:
                                for h in range(2):
                                    K.op("pe", lambda e, h=h: e.matmul(bk[:, 256 + h * P:256 + (h + 1) * P], lhsT=MPb[:, h, 128:256], rhs=MPb[:, h, 0:128], start=True, stop=True,
                                                                       skip_group_check=True), r=(R_MP,), w=(rk_,), inc=(h == 1))
                            K.op("dve", lambda e: e.tensor_tensor(out=Qh, in0=Qh, in1=v2(bk[:, 0:256]), op=ALU.add), x=(rk_,), r=(R_MP,), w=(R_MP,))
                            if not lastm:
                                K.op("dve", lambda e: e.tensor_tensor(out=NNb, in0=NNb, in1=v2(bk[:, 256:512]), op=ALU.add), x=(rk_,), r=(R_NN,), w=(R_NN,))
                            yield
                        if hit("rk_d2"):
                            K.op("dve", lambda e: e.tensor_copy(out=OT[:, 0, 0:512], in_=MR.rearrange("p a b -> p (a b)")), r=(R_MR,), w=(R_OT,))
                            K.op("dve", lambda e: e.tensor_copy(out=OT[:, 1, 0:512], in_=KR.rearrange("p a b -> p (a b)")), r=(R_KR,), w=(R_OT,))
                            K.op("dve", lambda e: e.tensor_copy(out=OT[:, 3, 0:512], in_=MPb.rearrange("p a b -> p (a b)")), r=(R_MP,), w=(R_OT,))
                            return
                        yield "seq"
                        K.op("dve", lambda e: e.tensor_scalar(out=Sb, in0=Sf, scalar1=gC[:, ch:ch + 1], scalar2=None, op0=ALU.mult), r=(R_Sf, R_gC), w=(R_Sb,))
                        bk, rk_ = K.bank()
                        for h in range(2):
                            hs = slice(64 * h, 64 * h + 64)
                            K.op("pe", lambda e, h=h, hs=hs: e.matmul(bk[:, h * 64:(h + 1) * 64], lhsT=AR[hs, ch, 0, :], rhs=Sb[hs, :], start=True, stop=False,
                                                                      skip_group_check=True), r=(R_AR, R_Sb), w=(rk_,), inc=False)
                            K.op("pe", lambda e, h=h, hs=hs: e.matmul(bk[:, h * 64:(h + 1) * 64], lhsT=KR[:, h, 0:128], rhs=vtm[:, gtile, hs], start=False, stop=True,
                                                                      skip_group_check=True), r=(R_KR, R_vtm), w=(rk_,), inc=(h == 1))
                        K.op("act", lambda e: e.activation(out=Wb, in_=v2(bk[:, 0:128]), func=AF.Copy), x=(rk_,), w=(R_Wb,))
                        bk, rk_ = K.bank()
                        for h in range(2):
                            K.op("pe", lambda e, h=h: e.matmul(bk[:, h * 64:(h + 1) * 64], lhsT=MPb[:, h, 128:256], rhs=Wb[:, h, :], start=True, stop=True,
                                                               skip_group_check=True), r=(R_MP, R_Wb), w=(rk_,), inc=(h == 1))
                        K.op("act", lambda e: e.activation(out=Ub, in_=v2(bk[:, 0:128]), func=AF.Copy), x=(rk_,), w=(R_Ub,))
                        bk, rk_ = K.bank()
                        for h in range(2):
                            hs = slice(64 * h, 64 * h + 64)
                            K.op("pe", lambda e, h=h, hs=hs: e.matmul(bk[hs, 0:64], lhsT=BKt[:, ch, 0, hs], rhs=Ub[:, h, :], start=True, stop=False, skip_group_check=True),
                                 r=(R_BKt, R_Ub), w=(rk_,), inc=False)
                            K.op("pe", lambda e, h=h, hs=hs: e.matmul(bk[hs, 0:64], lhsT=BKt[:, ch, 1, hs], rhs=vtm[:, gtile, hs], start=False, stop=True, skip_group_check=True),
                                 r=(R_BKt, R_vtm), w=(rk_,), inc=(h == 1))
                        for h in range(2):
                            hs = slice(64 * h, 64 * h + 64)
                            ob_, rob_ = (ops_, rops) if h == 0 else (ops2, rops2)
                            K.op("pe", lambda e, hs=hs: e.matmul(ob_[hs, csl], lhsT=Sb[hs, :], rhs=AR[hs, ch, 1, :], start=True, stop=False, skip_group_check=True),
                                 r=(R_Sb, R_AR), w=(rob_,), inc=False)
                            K.op("pe", lambda e, h=h, hs=hs: e.matmul(ob_[hs, csl], lhsT=Ub[:, h, :], rhs=MR[:, h, 128:256], start=False, stop=False, skip_group_check=True),
                                 r=(R_Ub, R_MR), w=(rob_,), inc=False)
                            K.op("pe", lambda e, h=h, hs=hs: e.matmul(ob_[hs, csl], lhsT=vtm[:, gtile, hs], rhs=KR[:, h, 128:256], start=False, stop=True, skip_group_check=True),
                                 r=(R_vtm, R_KR), w=(rob_,), inc=True)
                        K.op("dve", lambda e: e.scalar_tensor_tensor(out=Sf, in0=Sf, scalar=gC[:, ch:ch + 1], in1=bk[:, 0:64], op0=ALU.mult, op1=ALU.add),
                             x=(rk_,), r=(R_Sf, R_gC), w=(R_Sf,))
                        yield

                    order = list(range(4) if fwd else range(3, -1, -1))
                    gens = [chunk_gen(ch) for ch in order]
                    live = list(gens)
                    atseq = []
                    while live:
                        nl = []
                        for g in live:
                            try:
                                r_ = next(g)
                                if r_ == "seq":
                                    atseq.append(g)
                                else:
                                    nl.append(g)
                            except StopIteration:
                                pass
                        live = nl
                    for g in gens:
                        if g in atseq:
                            for _ in g:
                                pass
                    if hit("rk_d3"):
                        K.release(rops); K.release(rops2)
                        return "dbg_y"
                    K.op("act", lambda e: e.activation(out=T3[0:64, :], in_=ops_[0:64, :], func=AF.Copy), x=(rops,), w=(R_T3,))
                    K.op("act", lambda e: e.activation(out=T3[64:128, :], in_=ops2[64:128, :], func=AF.Copy), x=(rops2,), w=(R_T3,))
                    K.release(rops)
                    K.release(rops2)
                    K.op("dve", lambda e: e.scalar_tensor_tensor(out=BKt.rearrange("p a b c -> p (a b c)")[:, 0:512], in0=rT[:, tsl], scalar=rcol(2, b), in1=kd,
                                                                 op0=ALU.mult, op1=ALU.mult), r=(R_rT, R_kd, R_rkv), w=(R_BKt,))
                    ps, rb = K.bank()
                    K.mmg(ps, rb, [(bsum, BKt.rearrange("p a b c -> p (a b c)")[:, 0:512])], r=(R_BKt, R_cst2))
                    K.op("dve", lambda e: e.tensor_tensor(out=T4, in0=vT[:, tsl], in1=ps, op=ALU.mult), x=(rb,), r=(R_vT,), w=(R_T4,))
                    ps, rb = K.bank()
                    K.mmg(ps, rb, [(bmean_f, T3)], r=(R_T3, R_cst))
                    K.op("dve", lambda e: e.tensor_tensor(out=T3, in0=T3, in1=ps, op=ALU.subtract), x=(rb,), r=(R_T3,), w=(R_T3,))
                    T1f = T1.rearrange("p a b -> p (a b)")[:, 0:512]
                    T2f = T2.rearrange("p a b -> p (a b)")[:, 0:512]
                    K.op("act", lambda e: e.activation(out=T1f, in_=T3, func=AF.Square), r=(R_T3,), w=(R_T1,))
                    ps, rb = K.bank()
                    K.mmg(ps, rb, [(bmean_f, T1f)], r=(R_T1, R_cst))
                    K.op("act", lambda e: e.activation(out=T2f, in_=ps, func=AF.Ln, bias=epsc[:, 1:2]), x=(rb,), w=(R_T2,), r=(R_epsc,))
                    K.op("act", lambda e: e.activation(out=T2f, in_=T2f, func=AF.Exp, scale=-0.5), r=(R_T2,), w=(R_T2,))
                    K.op("dve", lambda e: e.tensor_tensor(out=T3, in0=T3, in1=T2f, op=ALU.mult), r=(R_T3, R_T2), w=(R_T3,))
                    K.op("dve", lambda e: e.tensor_scalar(out=T3, in0=T3, scalar1=rcol(3, b), scalar2=rcol(4, b), op0=ALU.mult, op1=ALU.add),
                         r=(R_T3, R_rkv), w=(R_T3,))
                    K.op("dve", lambda e: e.tensor_tensor(out=T3, in0=T3, in1=T4, op=ALU.add), r=(R_T3, R_T4), w=(R_T3,))
                    ps, rb = K.bank()
                    K.mmg(ps, rb, [(l2w[:, 2 + d, :], hg[:, d, tsl])], r=(R_l2w, R_hg))
                    if d == int(os.environ.get("RK_D0", "0")):
                        K.op("dve", lambda e: e.tensor_tensor(out=yT[:, b, tsl], in0=T3, in1=ps, op=ALU.mult), x=(rb,), r=(R_T3,), w=(R_yT,))
                    else:
                        K.op("dve", lambda e: e.tensor_tensor(out=T3, in0=T3, in1=ps, op=ALU.mult), x=(rb,), r=(R_T3,), w=(R_T3,))
                        K.op("dve", lambda e: e.tensor_tensor(out=yT[:, b, tsl], in0=yT[:, b, tsl], in1=T3, op=ALU.add), r=(R_T3, R_yT), w=(R_yT,))
                    if hit("rk_d4"):
                        return "dbg_y"

        if stop == "rk_y":
            return "dbg_y"
        K.barrier()
        A.top = mark_c
        usb2 = A.alloc(F32, [KC, 512])
        wo2 = wall[:, 0:KC * D].rearrange("p (k n) -> p k n", k=KC)
        K.dma("pool", wo2, wo2_d.rearrange("(b p) n -> p b n", p=P), w=(R_w[0],))
        for tb in range(4):
            for m in range(KC):
                ps, rb = K.bank()
                K.mmg(ps, rb, [(wo2[:, bb, m * P:(m + 1) * P], yT[:, bb, tb * 512:(tb + 1) * 512]) for bb in range(KC)], r=(R_w[0], R_yT))
                K.op("act", lambda e, m=m, ps=ps: e.activation(out=usb2[:, m, :], in_=ps, func=AF.Copy), x=(rb,), w=(R_usb,))
            tsl = slice(tb * 512, (tb + 1) * 512)

            def base_h2(src, rsrc, tsl=tsl):
                K.op("pool", lambda e: e.tensor_tensor(out=hT[:, :, tsl], in0=hT[:, :, tsl], in1=src, op=ALU.add), r=(rsrc, R_hT), w=(R_hT,))
            postnorm_add(1, 1, tb, 512, usb2, R_usb, base_h2)
        wi[0] = 2
        return None

    rr = rwkv()
    print("instr counts", K.cnt, "nsem", K.nsem)
    if rr == "dbg_y":
        return finish_debug(nc, es, K, A, y_d, OT, R_OT, ident_b, R_cstb, bf=True)
    if stop == "l1mix":
        return finish_debug(nc, es, K, A, y_d, hT, R_hT, ident_f, R_cst, bf=False)
    mlp(1)
    return finish_debug(nc, es, K, A, y_d, hT, R_hT, ident_f, R_cst, bf=False)


def finish_debug(nc, es, K, A, y_d, srcT, rsrc, ident, rid, bf):
    K.barrier()
    if A.top + 2 * 4096 > A.cap:
        A.top = A.cap - 2 * 4096 - 64
    ob = [A.alloc(F32, [D]) for _ in range(2)]
    R_ob = [Res("ob0"), Res("ob1")]
    R_y = Res("y")
    for tt in range(16):
        o, ro = ob[tt % 2], R_ob[tt % 2]
        for half in range(2):
            ps, rb = K.bank()
            if bf:
                K_terms = [(srcT[:, half * 4 + c4, tt * P:(tt + 1) * P], ident) for c4 in range(4)]
                for c4, (l, rh) in enumerate(K_terms):
                    K.op("pe", lambda e, l=l, rh=rh, c4=c4: e.matmul(ps[:, c4 * P:(c4 + 1) * P], lhsT=l, rhs=rh, start=True, stop=True,
                                                                     skip_group_check=True),
                         r=(rsrc, rid), w=(rb,), inc=(c4 == 3))
            else:
                for c4 in range(4):
                    c = half * 4 + c4
                    K.op("pe", lambda e, c=c, c4=c4: e.transpose(ps[:, c4 * P:(c4 + 1) * P], srcT[:, c, tt * P:(tt + 1) * P], ident),
                         r=(rsrc, rid), w=(rb,), inc=(c4 == 3))
            K.op("act" if half == 0 else "dve",
                 (lambda e, half=half, ps=ps: e.activation(out=o[:, half * 512:(half + 1) * 512], in_=ps, func=AF.Copy))
                 if half == 0 else
                 (lambda e, half=half, ps=ps: e.tensor_copy(out=o[:, half * 512:(half + 1) * 512], in_=ps)),
                 x=(rb,), w=(ro,))
        K.dma("sp", y_d[tt * P:(tt + 1) * P, :], o, r=(ro,), w=(R_y,))
    K.barrier()
    es.close()
    return nc


def make_tables(rel_table):
    rel_table = np.asarray(rel_table, np.float32)
    k = np.arange(P)[:, None]
    q = np.arange(P)[None, :]
    biasA = np.full((8, P, 896), NEG, np.float32)
    biasB = np.full((8, P, 384), NEG, np.float32)
    for r_, d in enumerate((1, 4, 16)):
        offs = (-1, 0, 1) if r_ < 2 else (0,)
        for oi, o in enumerate(offs):
            rel = k - q - P * o
            m = np.abs(rel) <= 64
            bk = t5_bucket_np(rel * d)
            col0 = (0, 384, 768)[r_] + oi * P
            for h in range(8):
                vals = rel_table[bk, h]
                biasA[h, :, col0:col0 + P] = np.where(m, vals, NEG)
    for oi, o in enumerate((-1, 0, 1)):
        rel = k - q - P * o
        m = np.abs(rel) <= 128
        bk = t5_bucket_np(rel)
        for h in range(8):
            vals = rel_table[bk, 8 + h]
            biasB[h, :, oi * P:(oi + 1) * P] = np.where(m, vals, NEG)
    return biasA, biasB


def make_consts():
    c = np.zeros((P, 4 * P), np.float32)
    c[:, 0:P] = np.eye(P)
    sw = np.zeros((P, P), np.float32)
    for m in range(P):
        sw[(m + 64) % P, m] = 1.0
    c[:, P:2 * P] = sw
    c[:, 2 * P:3 * P] = 1.0
    bm = np.zeros((P, P), np.float32)
    bm[0:64, 0:64] = 1.0 / 64
    bm[64:, 64:] = 1.0 / 64
    c[:, 3 * P:4 * P] = bm
    return c


_NC_CACHE = {}


def kernel(x, rel_table, norm_g, attn_w_in, attn_sink, attn_w_out,
           rk_mu_prev, rk_mu_next, rk_w_r, rk_w_k, rk_w_v, rk_w_o, rk_k_k, rk_k_a, rk_r_k,
           rk_gn_w, rk_gn_b, rk_w0, rk_w1, rk_w2, rk_a0, rk_a1, rk_a2, rk_g1, rk_g2,
           mlp_w1, mlp_w2, _stop=None, _cores=8):
    x = np.asarray(x, np.float32)
    biasA, biasB = make_tables(rel_table)
    gains = colvec(np.asarray(norm_g, np.float32).reshape(-1))
    sink = np.ascontiguousarray(np.broadcast_to(np.asarray(attn_sink, np.float32).reshape(1, 8), (P, 8)))
    f32 = lambda a: np.asarray(a, np.float32)
    cst2 = np.zeros((P, 11 * P), np.float32)
    pi = np.arange(P)[:, None]
    fi = np.arange(P)[None, :]
    cst2[:, 0:P] = pi < fi
    cst2[:, P:2 * P] = pi <= fi
    cst2[:, 2 * P:3 * P] = pi > fi
    cst2[:, 3 * P:4 * P] = pi >= fi
    cst2[:, 4 * P:5 * P] = (pi // 64) == (fi // 64)
    bd16 = (pi // 16) == (fi // 16)
    cst2[:, 5 * P:6 * P] = bd16
    for ni, n in enumerate((16, 32, 64)):
        cst2[:, (6 + ni) * P:(7 + ni) * P] = ((pi // (2 * n)) == (fi // (2 * n))) & ((pi // n) != (fi // n))
    cst2[:, 9 * P:10 * P] = (pi > fi) & bd16
    cst2[:, 10 * P:11 * P] = (pi < fi) & bd16
    rkv = np.concatenate([colvec(f32(v).reshape(-1)) for v in (
        rk_k_k[0], rk_k_a[0], rk_r_k[0], rk_gn_w[0], rk_gn_b[0], rk_w0[0, 0], rk_w0[0, 1], rk_a0[0, 0], rk_a0[0, 1])], axis=1)
    mu = np.concatenate([colvec(f32(m)[0, s_]) for s_ in range(6) for m in (rk_mu_prev, rk_mu_next)], axis=1)
    lora1 = np.concatenate([f32(rk_w1)[0, 0], f32(rk_w1)[0, 1], f32(rk_a1)[0, 0], f32(rk_a1)[0, 1],
                            f32(rk_g1)[0, 0], f32(rk_g1)[0, 1]], axis=1)
    common = {
        "cst2": cst2, "rkv": np.ascontiguousarray(rkv), "mu": np.ascontiguousarray(mu),
        "rk_w_r": np.ascontiguousarray(f32(rk_w_r)[0]), "rk_w_k": np.ascontiguousarray(f32(rk_w_k)[0]),
        "rk_w_v": np.ascontiguousarray(f32(rk_w_v)[0]), "rk_w_o": np.ascontiguousarray(f32(rk_w_o)[0]),
        "lora1": np.ascontiguousarray(lora1),
        "w2cat": np.ascontiguousarray(f32(rk_w2)[0].reshape(P, D)), "a2cat": np.ascontiguousarray(f32(rk_a2)[0].reshape(P, D)),
        "g2": np.ascontiguousarray(f32(rk_g2)[0]),
        "biasA": biasA, "biasB": biasB, "gains": gains, "sink": sink, "cst": make_consts(),
        "w_in": np.ascontiguousarray(np.asarray(attn_w_in, np.float32)[0]),
        "w_out": np.ascontiguousarray(np.asarray(attn_w_out, np.float32)[0]),
        "mlp_w1": np.ascontiguousarray(np.asarray(mlp_w1, np.float32)),
        "mlp_w2": np.ascontiguousarray(np.asarray(mlp_w2, np.float32)),
    }
    nc = build(_stop)
    in_maps = [dict(common, x=np.ascontiguousarray(x[b])) for b in range(_cores)]
    res = run_bass_kernel_spmd(nc, in_maps, core_ids=list(range(_cores)))
    return np.stack([np.asarray(r["y"], np.float32) for r in res.results], axis=0)
```

```python
import math
from contextlib import ExitStack

import numpy as np
import concourse.bass as bass
import concourse.mybir as mybir
from concourse.bass_utils import run_bass_kernel_spmd

F32 = mybir.dt.float32
BF16 = mybir.dt.bfloat16
AF = mybir.ActivationFunctionType
ALU = mybir.AluOpType

P = 128
S = 2048
D = 1024
KC = 8
DFF = 4096
NEG = -30000.0
EPS = 1e-6
GN_EPS = 64e-5
SEM_CH = 20000


class Res:
    __slots__ = ("name", "w", "readers", "dsem", "dval")

    def __init__(self, name):
        self.name = name
        self.w = None
        self.readers = {}
        self.dsem = None
        self.dval = 0


class KB:
    def __init__(self, nc, es):
        self.nc = nc
        self.es = es
        self.eng = {"pe": nc.tensor, "act": nc.scalar, "dve": nc.vector, "pool": nc.gpsimd, "sp": nc.sync}
        self.cnt = {e: 0 for e in self.eng}
        self.sems = {e: [] for e in self.eng}
        self.seen = {e: {} for e in self.eng}
        self.dres = []
        self.nsem = 0
        self.bank_i = 0
        self.held = set()
        self.pend = {}
        self.banks = []
        for i in range(8):
            t = es.enter_context(nc.psum_tensor(f"bank{i}", [P, 512], F32))
            self.banks.append((t[:, :], Res(f"bank{i}")))

    def new_sem(self, name):
        self.nsem += 1
        return self.es.enter_context(self.nc.semaphore(f"{name}_{self.nsem}"))

    def bank(self, hold=False):
        while (self.bank_i % 8) in self.held:
            self.bank_i += 1
        i = self.bank_i % 8
        self.bank_i += 1
        if hold:
            self.held.add(i)
        return self.banks[i]

    def release(self, bank_ap_res):
        for i, b in enumerate(self.banks):
            if b[1] is bank_ap_res:
                self.held.discard(i)

    def _deps(self, e, r, w, x):
        toks = []
        for res in r:
            if res.w is not None:
                toks.append(res.w[0])
        strict = e != "pe"
        for res in x:
            if res.w is not None:
                toks.append(res.w[0])
            for k, tok in res.readers.items():
                if k != e:
                    toks.append(tok)
        for res in w:
            if res.w is not None:
                if res.w[1] != e:
                    toks.append(res.w[0])
            for k, tok in res.readers.items():
                if k != e or strict:
                    toks.append(tok)
        return toks

    def _wait(self, e, toks):
        seen = self.seen[e]
        eng = self.eng[e]
        for tok in toks:
            sem, val, key = tok[0], tok[1], tok[2]
            if seen.get(key, 0) < val:
                eng.wait_ge(sem, val)
                seen[key] = val
            if len(tok) > 3:
                for k2, v2 in tok[3].items():
                    if seen.get(k2, 0) < v2:
                        seen[k2] = v2

    def _record(self, tok, e, r, w, x, rkey=None):
        k = rkey or e
        for res in w:
            res.w = (tok, e)
            res.readers = {}
        for res in r:
            res.readers[k] = tok
        for res in x:
            res.readers[k] = tok

    def op(self, e, fn, r=(), w=(), x=(), inc=True):
        self._wait(e, self._deps(e, r, w, x))
        ins = fn(self.eng[e])
        if not inc:
            pr, pw, px = self.pend.setdefault(e, ([], [], []))
            pr.extend(r); pw.extend(w); px.extend(x)
        if inc:
            if e in self.pend:
                pr, pw, px = self.pend.pop(e)
                r = tuple(dict.fromkeys(list(r) + pr))
                w = tuple(dict.fromkeys(list(w) + pw))
                x = tuple(dict.fromkeys(list(x) + px))
            i = self.cnt[e]
            self.cnt[e] += 1
            si, v = divmod(i, SEM_CH)
            while len(self.sems[e]) <= si:
                self.sems[e].append(self.new_sem(f"s{e}"))
            sem = self.sems[e][si]
            ins.then_inc(sem, 1)
            snap = dict(self.seen[e])
            snap[f"{e}{si}"] = v + 1
            tok = (sem, v + 1, f"{e}{si}", snap)
            self._record(tok, e, r, w, x)
        return ins

    def dma(self, q, out, in_, r=(), w=(), **kw):
        self._wait(q, self._deps("dma", r, w, ()))
        ins = self.eng[q].dma_start(out=out, in_=in_, **kw)
        tgt = w[0]
        if tgt.dsem is None:
            tgt.dsem = self.new_sem("d")
            self.dres.append(tgt)
        tgt.dval += 16
        ins.then_inc(tgt.dsem, 16)
        tok = (tgt.dsem, tgt.dval, f"d{id(tgt)}", dict(self.seen[q]))
        self._record(tok, "dma", r, w, (), rkey=f"dma{id(tgt)}")
        return ins

    def barrier(self):
        toks = []
        for e in self.eng:
            i = self.cnt[e]
            if i == 0:
                continue
            si, v = divmod(i - 1, SEM_CH)
            toks.append((self.sems[e][si], v + 1, f"{e}{si}"))
        for res in self.dres:
            toks.append((res.dsem, res.dval, f"d{id(res)}"))
        for e in self.eng:
            self._wait(e, toks)

    def mmg(self, out, bres, terms, r=(), skip=False, first=True, last=True):
        n = len(terms)
        for i, (l, rh) in enumerate(terms):
            fin = i == n - 1
            kw = {}
            if skip:
                kw["skip_group_check"] = True
            self.op("pe", lambda t, l=l, rh=rh, i=i, fin=fin: t.matmul(
                out, lhsT=l, rhs=rh, start=(first and i == 0), stop=(last and fin), **kw),
                r=r, w=(bres,), inc=fin)


class Arena:
    def __init__(self, nc, es, nbytes):
        self.t = es.enter_context(nc.sbuf_tensor("arena", [P, nbytes // 4], F32))
        self.top = 0
        self.cap = nbytes

    def alloc(self, dtype, shape):
        esz = 4 if dtype == F32 else 2
        n = 1
        for s in shape:
            n *= s
        nb = (n * esz + 63) // 64 * 64
        off = self.top
        self.top += nb
        assert self.top <= self.cap, f"SBUF arena overflow {self.top} > {self.cap}"
        ap = self.t[:, off // 4:(off + nb) // 4]
        if dtype != F32:
            ap = ap.bitcast(dtype)
        ap = ap[:, 0:n]
        if len(shape) == 2:
            ap = ap.rearrange("p (a b) -> p a b", a=shape[0])
        elif len(shape) == 3:
            ap = ap.rearrange("p (a b c) -> p a b c", a=shape[0], b=shape[1])
        return ap


def t5_bucket_np(rel):
    nb = 16
    max_exact = 8
    rel = np.asarray(rel, np.int64)
    bucket = np.where(rel > 0, nb, 0)
    n = np.abs(rel)
    nf = np.maximum(n, 1).astype(np.float32)
    large = max_exact + (np.log(nf / np.float32(max_exact)) / np.float32(math.log(1024 / max_exact))
                         * np.float32(nb - max_exact)).astype(np.int32)
    large = np.minimum(large, nb - 1)
    return bucket + np.where(n < max_exact, n, large)


def colvec(v):
    v = np.asarray(v, np.float32)
    return np.ascontiguousarray(v.reshape(-1, P).T)


def build(stop=None):
    nc = bass.Bass("TRN2", target_bir_lowering=False)
    es = ExitStack()
    dt = lambda name, shape: nc.dram_tensor(name, shape, F32, kind="ExternalInput").ap()
    x_d = dt("x", [S, D])
    biasA_d = dt("biasA", [8, P, 896])
    biasB_d = dt("biasB", [8, P, 384])
    gains_d = dt("gains", [P, 64])
    sink_d = dt("sink", [P, 8])
    cst_d = dt("cst", [P, 4 * P])
    w_in_d = dt("w_in", [D, 2304])
    w_out_d = dt("w_out", [D, D])
    w1_d = dt("mlp_w1", [2, D, DFF])
    w2_d = dt("mlp_w2", [2, DFF, D])
    cst2_d = dt("cst2", [P, 11 * P])
    rkv_d = dt("rkv", [P, 72])
    mu_d = dt("mu", [P, 96])
    wr_d = dt("rk_w_r", [D, D])
    wk_d = dt("rk_w_k", [D, D])
    wv_d = dt("rk_w_v", [D, D])
    wo2_d = dt("rk_w_o", [D, D])
    lora1_d = dt("lora1", [D, 512])
    w2c_d = dt("w2cat", [P, D])
    a2c_d = dt("a2cat", [P, D])
    g2_d = dt("g2", [2, P, D])
    y_d = nc.dram_tensor("y", [S, D], F32, kind="ExternalOutput").ap()

    K = KB(nc, es)
    A = Arena(nc, es, 206 * 1024)

    cst = A.alloc(F32, [4 * P])
    cstb = A.alloc(BF16, [4 * P])
    gains = A.alloc(F32, [64])
    esink = A.alloc(F32, [8])
    epsc = A.alloc(F32, [4])
    R_epsc = Res("epsc")
    R_cst, R_cstb, R_gains, R_esink = Res("cst"), Res("cstb"), Res("gains"), Res("esink")
    K.dma("sp", cst, cst_d, w=(R_cst,))
    K.dma("sp", gains, gains_d, w=(R_gains,))
    K.dma("sp", esink, sink_d, w=(R_esink,))
    K.op("act", lambda e: e.activation(out=esink, in_=esink, func=AF.Exp), r=(R_esink,), w=(R_esink,))
    K.op("dve", lambda e: e.tensor_copy(out=cstb, in_=cst), r=(R_cst,), w=(R_cstb,))
    ident_f = cst[:, 0:P]
    swap_f = cst[:, P:2 * P]
    ident_b = cstb[:, 0:P]
    ones_b = cstb[:, 2 * P:3 * P]

    uT = A.alloc(BF16, [KC, S])
    R_uT = Res("uT")
    X0 = A.top
    rstd = A.alloc(F32, [512])
    R_rstd = Res("rstd")
    sq = A.alloc(BF16, [KC, 512])
    R_sq = Res("sq")
    xin = [A.alloc(F32, [D]) for _ in range(2)]
    R_xin = [Res("xin0"), Res("xin1")]
    X1 = A.top
    WSLOT = 4
    W0 = A.top
    wall = A.alloc(BF16, [WSLOT * 4096])
    W1 = A.top
    wslot = [wall[:, i * 4096:(i + 1) * 4096] for i in range(WSLOT)]
    R_w = [Res(f"w{i}") for i in range(WSLOT)]
    wi = [0]
    mark_hT = A.top
    hT = A.alloc(F32, [KC, S])
    R_hT = Res("hT")
    mark_top = A.top
    OT = A.alloc(BF16, [KC, S])
    R_OT = Res("OT")
    mark_c = A.top

    def getw():
        i = wi[0] % WSLOT
        wi[0] += 1
        return wslot[i], R_w[i]

    def gcol(l, j, c):
        k = (l * 4 + j) * 8 + c
        return gains[:, k:k + 1]

    def rstd_from(ps, bres, n, out, rout, scale=1.0 / D, eps=EPS):
        K.op("act", lambda e: e.activation(out=out, in_=ps, func=AF.Ln, scale=scale, bias=eps_ap(eps)),
             x=(bres,), w=(rout,), r=(R_epsc,))
        K.op("act", lambda e: e.activation(out=out, in_=out, func=AF.Exp, scale=-0.5), r=(rout,), w=(rout,))

    K.op("pool", lambda e: e.memset(epsc[:, 0:1], EPS), w=(R_epsc,))
    K.op("pool", lambda e: e.memset(epsc[:, 1:2], GN_EPS), r=(), w=(R_epsc,))
    K.op("pool", lambda e: e.memset(epsc[:, 2:3], 1e-18), r=(), w=(R_epsc,))

    def eps_ap(eps):
        return epsc[:, 0:1] if eps == EPS else epsc[:, 1:2]

    def load_x_tile(tt):
        b = tt % 2
        K.dma("sp", xin[b], x_d[tt * P:(tt + 1) * P, :], w=(R_xin[b],))
        return xin[b], R_xin[b]

    A.top = mark_hT
    hblk = A.alloc(F32, [KC, 512])
    R_hblk = Res("hblk")

    def xT_block(tb, dst, rdst):
        for t4 in range(4):
            tt = tb * 4 + t4
            xt, rx = load_x_tile(tt)
            for half in range(2):
                ps, rb = K.bank()
                for c4 in range(4):
                    c = half * 4 + c4
                    K.op("pe", lambda e, c=c, c4=c4: e.transpose(ps[:, c4 * P:(c4 + 1) * P], xt[:, c * P:(c + 1) * P], ident_f),
                         r=(rx, R_cst), w=(rb,), inc=(c4 == 3))
                K.op("act" if half == 0 else "dve",
                     (lambda e, half=half, t4=t4: e.activation(out=dst[:, half * 4:half * 4 + 4, t4 * P:(t4 + 1) * P],
                                                               in_=ps.rearrange("p (a b) -> p a b", a=4), func=AF.Copy))
                     if half == 0 else
                     (lambda e, half=half, t4=t4: e.tensor_copy(out=dst[:, half * 4:half * 4 + 4, t4 * P:(t4 + 1) * P],
                                                                in_=ps.rearrange("p (a b) -> p a b", a=4))),
                     x=(rb,), w=(rdst,))

    def prenorm_block(src, rsrc, l, j, dst_uT, ntok=512):
        K.op("act", lambda e: e.activation(out=sq[:, :, 0:ntok], in_=src, func=AF.Square), r=(rsrc,), w=(R_sq,))
        ps, rb = K.bank()
        K.mmg(ps[:, 0:ntok], rb, [(ones_b, sq[:, c, 0:ntok]) for c in range(KC)], r=(R_sq, R_cstb))
        rstd_from(ps[:, 0:ntok], rb, ntok, rstd[:, 0:ntok], R_rstd)
        for c in range(KC):
            K.op("dve", lambda e, c=c: e.scalar_tensor_tensor(out=dst_uT[:, c, :], in0=src[:, c, :], scalar=gcol(l, j, c),
                                                              in1=rstd[:, 0:ntok], op0=ALU.mult, op1=ALU.mult),
                 r=(rsrc, R_rstd, R_gains), w=(R_uT,))

    for tb in range(4):
        xT_block(tb, hblk, R_hblk)
        prenorm_block(hblk, R_hblk, 0, 0, uT[:, :, tb * 512:(tb + 1) * 512])

    if stop == "uT":
        return finish_debug(nc, es, K, A, y_d, uT, R_uT, ident_b, R_cstb, bf=True)

    K.barrier()
    A.top = mark_hT
    qT = A.alloc(BF16, [S])
    kT = A.alloc(BF16, [S])
    R_qT, R_kT = Res("qT"), Res("kT")
    vaug = [A.alloc(BF16, [3, 16, P]) for _ in range(2)]
    R_vaug = [Res("vaug0"), Res("vaug1")]
    acc = A.alloc(F32, [S])
    R_acc = Res("acc")
    PTB = 4
    ptb = [A.alloc(BF16, [384]) for _ in range(PTB)]
    R_pt = [Res(f"pt{i}") for i in range(PTB)]
    pti = [0]
    bias_sb = A.alloc(BF16, [2, 896])
    R_bias = Res("bias")
    K.op("pool", lambda e: e.memset(vaug[0][:, :, :, 64:128], 1.0), w=(R_vaug[0],))
    K.op("pool", lambda e: e.memset(vaug[1][:, :, :, 0:64], 1.0), w=(R_vaug[1],))

    w_in_v = w_in_d.rearrange("(kc p) n -> p kc n", p=P)

    def proj_fm(wap, dst, rdst, rw, scale):
        for tb in range(4):
            ps, rb = K.bank()
            K.mmg(ps, rb, [(wap[:, kc, :], uT[:, kc, tb * 512:(tb + 1) * 512]) for kc in range(KC)], r=(rw, R_uT))
            K.op("act", lambda e, tb=tb: e.activation(out=dst[:, tb * 512:(tb + 1) * 512], in_=ps, func=AF.Copy, scale=scale),
                 x=(rb,), w=(rdst,))

    def tok_slice(d, i):
        L = S // d
        n0 = P * i
        c = n0 // L
        p0 = n0 - c * L
        return slice(c + d * p0, c + d * (p0 + P - 1) + 1, d)

    def proj_v(wv, rw, orders):
        for r_, d in orders:
            for g in range(4):
                ps, rb = K.bank()
                for i4 in range(4):
                    i = g * 4 + i4
                    ts = tok_slice(d, i)
                    K.mmg(ps[:, i4 * P:(i4 + 1) * P], rb, [(uT[:, kc, ts], wv[:, kc, :]) for kc in range(KC)],
                          r=(rw, R_uT), skip=True, first=True)
                psv = ps.rearrange("p (a b) -> p a b", a=4)
                K.op("dve", lambda e, r_=r_, g=g: e.tensor_copy(out=vaug[0][:, r_, g * 4:g * 4 + 4, 0:64], in_=psv[:, :, 0:64]),
                     x=(rb,), w=(R_vaug[0],))
                K.op("act", lambda e, r_=r_, g=g: e.activation(out=vaug[1][:, r_, g * 4:g * 4 + 4, 64:128], in_=psv[:, :, 64:128],
                                                              func=AF.Copy), x=(rb,), w=(R_vaug[1],))

    def banded(hh, r_, d, bias_off, first_branch):
        p0 = 64 * hh
        L = S // d
        QB = min(512, L)
        nqt = QB // P
        ntile = L // P
        its = []
        for c in range(d):
            for q0t in range(0, ntile, nqt):
                kts = [kt for kt in range(q0t - 1, q0t + nqt + 1) if 0 <= kt < ntile]
                for ki, kt in enumerate(kts):
                    its.append((c, q0t, kt, ki == 0, ki == len(kts) - 1))
        state = {}

        def emit_scores(it):
            c, q0t, kt, first, last = it
            qa = max(kt - 1, q0t)
            qb_ = min(kt + 1, q0t + nqt - 1)
            ncol = (qb_ - qa + 1) * P
            boff = bias_off + (qa - kt + 1) * P
            ps, rb = K.bank()
            ksl = slice(c + d * kt * P, c + d * (kt * P + P - 1) + 1, d)
            qsl = slice(c + d * qa * P, c + d * ((qb_ + 1) * P - 1) + 1, d)
            K.mmg(ps[:, 0:ncol], rb,
                  [(ident_b, bias_sb[:, hh, boff:boff + ncol]),
                   (kT[p0:p0 + 64, ksl], qT[p0:p0 + 64, qsl])],
                  r=(R_cstb, R_bias, R_kT, R_qT))
            pi = pti[0] % PTB
            pti[0] += 1
            pt, rpt = ptb[pi], R_pt[pi]
            K.op("act", lambda e: e.activation(out=pt[:, 0:ncol], in_=ps[:, 0:ncol], func=AF.Exp), x=(rb,), w=(rpt,))
            return (pt, rpt, ncol, qa)

        def emit_pv(it, sc):
            c, q0t, kt, first, last = it
            pt, rpt, ncol, qa = sc
            if first:
                state["pv"] = K.bank(hold=True)
            pv, rpv = state["pv"]
            tile_idx = c * ntile + kt
            o0 = (qa - q0t) * P
            K.mmg(pv[:, o0:o0 + ncol], rpv, [(vaug[hh][:, r_, tile_idx, :], pt[:, 0:ncol])],
                  r=(rpt, R_vaug[hh]), skip=True, first=first, last=last)
            if last:
                t0 = c + d * q0t * P
                asl = slice(t0, t0 + d * (QB - 1) + 1, d)
                if first_branch:
                    K.op("dve", lambda e: e.tensor_copy(out=acc[:, asl], in_=pv[:, 0:QB]), x=(rpv,), w=(R_acc,))
                else:
                    K.op("dve", lambda e: e.tensor_tensor(out=acc[:, asl], in0=acc[:, asl], in1=pv[:, 0:QB], op=ALU.add),
                         x=(rpv,), r=(R_acc,), w=(R_acc,))
                K.release(rpv)

        prev = None
        for it in its:
            sc = emit_scores(it)
            if prev is not None:
                emit_pv(*prev)
            prev = (it, sc)
        emit_pv(*prev)

    def normalize(hh, blk, sink_col=None):
        p0 = 64 * hh
        q0 = 64 - p0
        if sink_col is not None:
            K.op("dve", lambda e: e.tensor_scalar(out=acc[q0:q0 + 64, :], in0=acc[q0:q0 + 64, :], scalar1=esink[q0:q0 + 64, sink_col:sink_col + 1],
                                                  scalar2=None, op0=ALU.add), r=(R_acc, R_esink), w=(R_acc,))
        K.op("dve", lambda e: e.reciprocal(out=acc[q0:q0 + 64, :], in_=acc[q0:q0 + 64, :]), r=(R_acc,), w=(R_acc,))
        for tb in range(4):
            ps, rb = K.bank()
            K.mmg(ps, rb, [(swap_f, acc[:, tb * 512:(tb + 1) * 512])], r=(R_cst, R_acc))
            K.op("dve", lambda e, tb=tb, ps=ps: e.tensor_tensor(out=OT[p0:p0 + 64, blk, tb * 512:(tb + 1) * 512],
                                                               in0=acc[p0:p0 + 64, tb * 512:(tb + 1) * 512], in1=ps[p0:p0 + 64, :], op=ALU.mult),
                 x=(rb,), r=(R_acc,), w=(R_OT,))

    for j in range(4):
        wt, rw = getw()
        wv3 = wt[:, 0:KC * 384].rearrange("p (k g n) -> p k g n", k=KC, g=3)
        for g3 in range(3):
            K.dma("pool", wv3[:, :, g3, :], w_in_v[:, :, g3 * 512 + j * P:g3 * 512 + (j + 1) * P], w=(rw,))
        K.dma("pool", bias_sb, biasA_d[2 * j:2 * j + 2].rearrange("h p n -> p h n"), w=(R_bias,))
        proj_fm(wv3[:, :, 0, :], qT, R_qT, rw, 0.125)
        proj_fm(wv3[:, :, 1, :], kT, R_kT, rw, 1.0)
        proj_v(wv3[:, :, 2, :], rw, [(0, 1), (1, 4), (2, 16)])
        for hh in range(2):
            banded(hh, 0, 1, 0, True)
            banded(hh, 1, 4, 384, False)
            banded(hh, 2, 16, 768 - P, False)
            normalize(hh, j)

    if stop == "attnA":
        return finish_debug(nc, es, K, A, y_d, OT, R_OT, ident_b, R_cstb, bf=True)

    wt, rw = getw()
    wkv = wt[:, 0:KC * 256].rearrange("p (k g n) -> p k g n", k=KC, g=2)
    for g2 in range(2):
        K.dma("pool", wkv[:, :, g2, :], w_in_v[:, :, 2048 + g2 * P:2048 + (g2 + 1) * P], w=(rw,))
    proj_fm(wkv[:, :, 0, :], kT, R_kT, rw, 1.0)
    proj_v(wkv[:, :, 1, :], rw, [(0, 1)])
    for jb in range(4):
        wt, rw = getw()
        wq = wt[:, 0:KC * P].rearrange("p (k g n) -> p k g n", k=KC, g=2)
        for g2 in range(2):
            c0 = 1536 + (jb + 4 * g2) * 64
            K.dma("pool", wq[:, :, g2, :], w_in_v[:, :, c0:c0 + 64], w=(rw,))
        K.dma("pool", bias_sb[:, :, 0:384], biasB_d[jb:jb + 5:4].rearrange("h p n -> p h n"), w=(R_bias,))
        proj_fm(wt[:, 0:KC * P].rearrange("p (k n) -> p k n", k=KC), qT, R_qT, rw, 0.125)
        for hh in range(2):
            banded(hh, 0, 1, 0, True)
            normalize(hh, 4 + jb, sink_col=jb + 4 * hh)

    if stop == "attn":
        return finish_debug(nc, es, K, A, y_d, OT, R_OT, ident_b, R_cstb, bf=True)

    K.barrier()
    A.top = mark_c
    usb = A.alloc(F32, [KC, 512])
    R_usb = Res("usb")
    wo = wall[:, 0:KC * D].rearrange("p (k n) -> p k n", k=KC)
    R_wo = R_w[0]
    wi[0] = 2
    K.dma("pool", wo[:, 0:4, :], w_out_d[0:512, :].rearrange("(b p) n -> p b n", p=P), w=(R_wo,))
    wob = w_out_d[512:1024, :].rearrange("(g jb q) n -> q g jb n", g=2, jb=4)
    K.dma("pool", wo[0:64, 4:8, :], wob[:, 0], w=(R_wo,))
    K.dma("pool", wo[64:128, 4:8, :], wob[:, 1], w=(R_wo,))

    def postnorm_add(l, j, tb, ntok, src, rsrc, base_fn):
        K.op("act", lambda e: e.activation(out=sq[:, :, 0:ntok], in_=src, func=AF.Square), r=(rsrc,), w=(R_sq,))
        ps, rb = K.bank()
        K.mmg(ps[:, 0:ntok], rb, [(ones_b, sq[:, c, 0:ntok]) for c in range(KC)], r=(R_sq, R_cstb))
        rstd_from(ps[:, 0:ntok], rb, ntok, rstd[:, 0:ntok], R_rstd)
        if stop == "pn":
            for c in range(KC):
                K.op("dve", lambda e, c=c: e.scalar_tensor_tensor(out=hT[:, c, tb * 512:(tb + 1) * 512], in0=src[:, c, :], scalar=gcol(l, j, c),
                                                                  in1=rstd[:, 0:ntok], op0=ALU.mult, op1=ALU.mult),
                     r=(rsrc, R_rstd, R_gains), w=(R_hT,))
            return
        for c in range(KC):
            K.op("dve", lambda e, c=c: e.scalar_tensor_tensor(out=src[:, c, :], in0=src[:, c, :], scalar=gcol(l, j, c),
                                                              in1=rstd[:, 0:ntok], op0=ALU.mult, op1=ALU.mult),
                 r=(rsrc, R_rstd, R_gains), w=(rsrc,))
        base_fn(src, rsrc)

    def outproj_block(tb, wmat, rwm, inT, rin, nblk):
        for m in range(KC):
            ps, rb = K.bank()
            K.mmg(ps, rb, [(wmat[:, b, m * P:(m + 1) * P], inT[:, b, tb * 512:(tb + 1) * 512]) for b in range(nblk)], r=(rwm, rin))
            K.op("act", lambda e, m=m, ps=ps: e.activation(out=usb[:, m, :], in_=ps, func=AF.Copy), x=(rb,), w=(R_usb,))

    for tb in range(4):
        outproj_block(tb, wo, R_wo, OT, R_OT, 8)
        if stop == "um":
            K.op("dve", lambda e: e.tensor_copy(out=hT[:, :, tb * 512:(tb + 1) * 512], in_=usb), r=(R_usb,), w=(R_hT,))
            continue

        def base_x(src, rsrc, tb=tb):
            for t4 in range(4):
                tt = tb * 4 + t4
                xt, rx = load_x_tile(tt)
                for half in range(2):
                    ps, rb = K.bank()
                    for c4 in range(4):
                        c = half * 4 + c4
                        K.op("pe", lambda e, c=c, c4=c4: e.transpose(ps[:, c4 * P:(c4 + 1) * P], xt[:, c * P:(c + 1) * P], ident_f),
                             r=(rx, R_cst), w=(rb,), inc=(c4 == 3))
                    K.op("dve", lambda e, half=half, t4=t4, ps=ps: e.tensor_tensor(
                        out=hT[:, half * 4:half * 4 + 4, tb * 512 + t4 * P:tb * 512 + (t4 + 1) * P],
                        in0=src[:, half * 4:half * 4 + 4, t4 * P:(t4 + 1) * P],
                        in1=ps.rearrange("p (a b) -> p a b", a=4), op=ALU.add),
                        x=(rb,), r=(rsrc,), w=(R_hT,))
        if stop == "pn":
            def base_dbg(src, rsrc, tb=tb):
                K.op("dve", lambda e: e.tensor_copy(out=hT[:, :, tb * 512:(tb + 1) * 512], in_=src), r=(rsrc,), w=(R_hT,))
            postnorm_add(0, 1, tb, 512, usb, R_usb, base_dbg)
        else:
            postnorm_add(0, 1, tb, 512, usb, R_usb, base_x)

    if stop in ("l0attn", "um", "pn"):
        return finish_debug(nc, es, K, A, y_d, hT, R_hT, ident_f, R_cst, bf=False)

    R_wx = [Res("wx0"), Res("wx1")]
    w6i = [0]

    def mlp(l):
        K.barrier()
        A.top = mark_top
        acc2 = A.alloc(F32, [KC, 1024])
        R_acc2 = Res("acc2")
        h1 = [A.alloc(BF16, [4, 1024]) for _ in range(2)]
        R_h1 = [Res("h1a"), Res("h1b")]
        rl = [A.alloc(BF16, [512]) for _ in range(2)]
        R_rl = [Res("rl0"), Res("rl1")]
        w1v = w1_d[l].rearrange("(kc p) n -> p kc n", p=P)
        uflat = uT.rearrange("p k n -> p (k n)")
        uTm = uflat[:, 0:KC * 1024].rearrange("p (k n) -> p k n", k=KC)
        slots6 = wslot + [uflat[:, 8192:12288], uflat[:, 12288:16384]]
        R_slots6 = R_w + R_wx

        def getw6():
            i = w6i[0] % 6
            w6i[0] += 1
            return slots6[i], R_slots6[i]
        w2v = w2_d[l].rearrange("(g kc p) n -> g p kc n", kc=4, p=P)
        rli = 0
        for th in range(2):
            t0 = th * 1024
            for ts in range(2):
                prenorm_block(hT[:, :, t0 + ts * 512:t0 + (ts + 1) * 512], R_hT, l, 2, uTm[:, :, ts * 512:(ts + 1) * 512])
            def loadw(g):
                wa, rwa = getw6()
                wb, rwb = getw6()
                w1g = wa.rearrange("p (k n) -> p k n", k=KC)
                w2g = wb.rearrange("p (k n) -> p k n", k=4)
                K.dma("pool", w1g, w1v[:, :, g * 512:(g + 1) * 512], w=(rwa,))
                K.dma("pool", w2g, w2v[g], w=(rwb,))
                return w1g, rwa, w2g, rwb
            pre = [loadw(0), loadw(1)]
            for g in range(8):
                w1g, rwa, w2g, rwb = pre.pop(0)
                if g + 2 < 8:
                    pre.append(loadw(g + 2))
                hb, rhb = h1[g % 2], R_h1[g % 2]
                for fb in range(4):
                    for ts in range(2):
                        ps, rb = K.bank()
                        K.mmg(ps, rb, [(w1g[:, kc, fb * P:(fb + 1) * P], uTm[:, kc, ts * 512:(ts + 1) * 512]) for kc in range(KC)],
                              r=(rwa, R_uT))
                        rt, rrt = rl[rli % 2], R_rl[rli % 2]
                        rli += 1
                        K.op("act", lambda e, rt=rt, ps=ps: e.activation(out=rt, in_=ps, func=AF.Relu), x=(rb,), w=(rrt,))
                        K.op("dve", lambda e, rt=rt, fb=fb, ts=ts, hb=hb: e.tensor_tensor(out=hb[:, fb, ts * 512:(ts + 1) * 512], in0=rt, in1=rt, op=ALU.mult),
                             r=(rrt,), w=(rhb,))
                for m in range(KC):
                    for ts in range(2):
                        ps, rb = K.bank()
                        K.mmg(ps, rb, [(w2g[:, fb, m * P:(m + 1) * P], hb[:, fb, ts * 512:(ts + 1) * 512]) for fb in range(4)],
                              r=(rwb, rhb))
                        dst = acc2[:, m, ts * 512:(ts + 1) * 512]
                        if g == 0:
                            K.op("act", lambda e, dst=dst, ps=ps: e.activation(out=dst, in_=ps, func=AF.Copy), x=(rb,), w=(R_acc2,))
                        else:
                            K.op("dve", lambda e, dst=dst, ps=ps: e.tensor_tensor(out=dst, in0=dst, in1=ps, op=ALU.add),
                                 x=(rb,), r=(R_acc2,), w=(R_acc2,))
            for ts in range(2):
                tsl = slice(t0 + ts * 512, t0 + (ts + 1) * 512)

                def base_h(src, rsrc, tsl=tsl):
                    K.op("dve", lambda e: e.tensor_tensor(out=hT[:, :, tsl], in0=hT[:, :, tsl], in1=src, op=ALU.add),
                         r=(rsrc, R_hT), w=(R_hT,))
                postnorm_add(l, 3, None, 512, acc2[:, :, ts * 512:(ts + 1) * 512], R_acc2, base_h)

    mlp(0)
    if stop == "l0":
        return finish_debug(nc, es, K, A, y_d, hT, R_hT, ident_f, R_cst, bf=False)


    def rwkv():
        E05 = math.exp(-0.5)
        import os
        skipc = {}
        def hit(name):
            if stop != name:
                return False
            skipc[name] = skipc.get(name, 0) + 1
            return skipc[name] > int(os.environ.get("RK_SKIP", "0"))
        K.barrier()
        for tb in range(4):
            prenorm_block(hT[:, :, tb * 512:(tb + 1) * 512], R_hT, 1, 0, uT[:, :, tb * 512:(tb + 1) * 512])
        K.barrier()
        yT = OT
        R_yT = R_OT
        A.top = X0
        hw = A.alloc(BF16, [S]); ha = A.alloc(BF16, [S]); hg = A.alloc(BF16, [2, S])
        R_hw, R_ha, R_hg = Res("hw"), Res("ha"), Res("hg")
        coef = A.alloc(F32, [6, 3, 8]); rkv = A.alloc(F32, [72]); mu = A.alloc(F32, [96]); gC = A.alloc(F32, [4])
        R_coef, R_rkv, R_mu, R_gC = Res("coef"), Res("rkv"), Res("mu"), Res("gC")
        assert A.top <= X1, (A.top, X1)
        A.top = W0
        chunkbase = A.top
        stg = A.alloc(F32, [KC, P]); R_stg = Res("stg")
        wsl = [A.alloc(BF16, [3, KC, P]) for _ in range(2)]; R_wsl = [Res("ws0"), Res("ws1")]
        wsend = A.top
        wsi = [0]
        A.top = chunkbase
        MRc = [A.alloc(BF16, [2, 256]) for _ in range(4)]; R_MRc = [Res(f"MR{i}") for i in range(4)]
        KRc = [A.alloc(BF16, [2, 256]) for _ in range(4)]; R_KRc = [Res(f"KR{i}") for i in range(4)]
        MPc = [A.alloc(BF16, [2, 256]) for _ in range(4)]; R_MPc = [Res(f"MP{i}") for i in range(4)]
        NNc = [A.alloc(BF16, [2, P]) for _ in range(4)]; R_NNc = [Res(f"NN{i}") for i in range(4)]
        Wbc = [A.alloc(BF16, [2, 64]) for _ in range(4)]; R_Wbc = [Res(f"Wb{i}") for i in range(4)]
        Ubc = [A.alloc(BF16, [2, 64]) for _ in range(4)]; R_Ubc = [Res(f"Ub{i}") for i in range(4)]
        A.top = max(A.top, wsend)
        l2w = A.alloc(BF16, [4, P]); R_l2w = Res("l2w")
        T3 = A.alloc(F32, [512]); T4 = A.alloc(F32, [512]); R_T3, R_T4 = Res("T3"), Res("T4")
        AR = A.alloc(BF16, [4, 2, P]); R_AR = Res("AR")
        BKf = A.alloc(BF16, [2, 512]); R_BKf = Res("BKf")
        kd = A.alloc(BF16, [512]); R_kd = Res("kd")
        BKt = A.alloc(BF16, [4, 2, P]); R_BKt = Res("BKt")
        Sf = A.alloc(F32, [64]); Sb = A.alloc(BF16, [64]); R_Sf, R_Sb = Res("Sf"), Res("Sb")
        cst2 = A.alloc(BF16, [11 * P]); R_cst2 = Res("cst2")
        assert A.top <= W1, (A.top, W1)
        A.top = mark_c
        rT = A.alloc(BF16, [S]); kT2 = A.alloc(BF16, [S]); vT = A.alloc(BF16, [S]); vtm = A.alloc(BF16, [16, P]); kkT = A.alloc(BF16, [S])
        R_rT, R_kT2, R_vT, R_vtm, R_kkT = Res("rT"), Res("kT2"), Res("vT"), Res("vtm"), Res("kkT")
        T1 = A.alloc(F32, [4, 129]); T2 = A.alloc(F32, [4, 129]); R_T1, R_T2 = Res("T1"), Res("T2")

        K.dma("pool", cst2, cst2_d, w=(R_cst2,))
        K.dma("sp", rkv, rkv_d, w=(R_rkv,))
        K.dma("sp", mu, mu_d, w=(R_mu,))
        LT2 = cst2[:, 0:2 * P]
        GT2 = cst2[:, 2 * P:4 * P]
        LTm = cst2[:, 0:P]
        GTm = cst2[:, 2 * P:3 * P]
        bsum = cst2[:, 4 * P:5 * P]
        bmean_f = cst[:, 3 * P:4 * P]
        ones_f = cst[:, 2 * P:3 * P]
        BD16 = cst2[:, 5 * P:6 * P]
        OFFS = cst2[:, 6 * P:9 * P].rearrange("p (a b) -> p a b", a=3)
        NDf = cst2[:, 9 * P:10 * P]
        NDr = cst2[:, 10 * P:11 * P]
        def mo_view(t):
            return t.bitcast(BF16)[:, 0:768].rearrange("p (a b c) -> p a b c", a=3, b=2)
        T1fl = T1.rearrange("p a b -> p (a b)")
        T2fl = T2.rearrange("p a b -> p (a b)")
        MOc = [mo_view(T3), mo_view(T4), mo_view(T1fl), mo_view(T2fl)]
        R_MOc = [R_T3, R_T4, R_T1, R_T2]
        muv = mu.rearrange("p (s t c) -> p s t c", s=6, t=2)
        K.op("dve", lambda e: e.tensor_copy(out=coef[:, :, 1:3, :], in_=muv), r=(R_mu,), w=(R_coef,))
        K.op("dve", lambda e: e.tensor_tensor(out=coef[:, :, 0, :], in0=muv[:, :, 0, :], in1=muv[:, :, 1, :], op=ALU.add), r=(R_mu,), w=(R_coef,))
        K.op("dve", lambda e: e.tensor_scalar(out=coef[:, :, 0, :], in0=coef[:, :, 0, :], scalar1=-1.0, scalar2=1.0, op0=ALU.mult, op1=ALU.add),
             r=(R_coef,), w=(R_coef,))

        def rcol(i, b):
            return rkv[:, i * 8 + b:i * 8 + b + 1]

        def load_scaled(wsrc, stream):
            K.dma("sp", stg, wsrc.rearrange("(kc p) n -> p kc n", p=P), w=(R_stg,))
            wsi[0] += 1
            ws, R_ws = wsl[wsi[0] % 2], R_wsl[wsi[0] % 2]
            for t in range(3):
                K.op("dve", lambda e, t=t: e.tensor_tensor(out=ws[:, t], in0=stg, in1=coef[:, stream, t, :].unsqueeze(2).to_broadcast([P, KC, P]), op=ALU.mult),
                     r=(R_stg, R_coef), w=(R_ws,))

        def proj3(evac):
            ws, R_ws = wsl[wsi[0] % 2], R_wsl[wsi[0] % 2]
            for tb in range(4):
                t0 = tb * 512
                ps, rb = K.bank()
                terms = [(ws[:, 0, kc, :], uT[:, kc, t0:t0 + 512], ps) for kc in range(KC)]
                if tb == 0:
                    terms += [(ws[:, 1, kc, :], uT[:, kc, 0:511], ps[:, 1:512]) for kc in range(KC)]
                else:
                    terms += [(ws[:, 1, kc, :], uT[:, kc, t0 - 1:t0 + 511], ps) for kc in range(KC)]
                if tb == 3:
                    terms += [(ws[:, 2, kc, :], uT[:, kc, t0 + 1:S], ps[:, 0:511]) for kc in range(KC)]
                else:
                    terms += [(ws[:, 2, kc, :], uT[:, kc, t0 + 1:t0 + 513], ps) for kc in range(KC)]
                import os as _os
                terms = terms[:int(_os.environ.get("RK_NT", "24"))]
                n = len(terms)
                for i, (l, rh, o) in enumerate(terms):
                    K.op("pe", lambda e, l=l, rh=rh, o=o, i=i: e.matmul(o, lhsT=l, rhs=rh, start=(i == 0), stop=(i == n - 1), skip_group_check=True),
                         r=(R_ws, R_uT), w=(rb,), inc=(i == n - 1))
                evac(tb, ps, rb)

        if stop == "rk_c":
            return "dbg_y"
        load_scaled(lora1_d[:, 0:128], 1)
        if stop == "rk_ls":
            return "dbg_y"
        proj3(lambda tb, ps, rb: K.op("act", lambda e: e.activation(out=hw[:, tb * 512:(tb + 1) * 512], in_=ps, func=AF.Tanh), x=(rb,), w=(R_hw,)))
        load_scaled(lora1_d[:, 128:256], 4)
        proj3(lambda tb, ps, rb: K.op("act", lambda e: e.activation(out=ha[:, tb * 512:(tb + 1) * 512], in_=ps, func=AF.Copy), x=(rb,), w=(R_ha,)))
        for d in range(2):
            load_scaled(lora1_d[:, 256 + d * 128:256 + (d + 1) * 128], 5)
            proj3(lambda tb, ps, rb, d=d: K.op("act", lambda e: e.activation(out=hg[:, d, tb * 512:(tb + 1) * 512], in_=ps, func=AF.Sigmoid), x=(rb,), w=(R_hg,)))

        if stop == "rk_pre0":
            return "dbg_y"
        if stop == "rk_pre":
            K.op("dve", lambda e: e.tensor_copy(out=OT[:, 0, :], in_=hw), r=(R_hw,), w=(R_OT,))
            K.op("dve", lambda e: e.tensor_copy(out=OT[:, 1, :], in_=ha), r=(R_ha,), w=(R_OT,))
            K.op("dve", lambda e: e.tensor_copy(out=OT[:, 2:4, :], in_=hg), r=(R_hg,), w=(R_OT,))
            return "dbg_y"
        NBLK = int(os.environ.get("RK_NB", "8"))
        NDIR = int(os.environ.get("RK_ND", "2"))
        for b in range(NBLK):
            bc = slice(b * P, (b + 1) * P)
            K.barrier()
            K.dma("pool", l2w[:, 0, :], w2c_d[:, bc], w=(R_l2w,))
            K.dma("pool", l2w[:, 1, :], a2c_d[:, bc], w=(R_l2w,))
            K.dma("pool", l2w[:, 2:4, :], g2_d[:, :, bc].rearrange("d p n -> p d n"), w=(R_l2w,))
            load_scaled(wr_d[:, bc], 0)
            proj3(lambda tb, ps, rb: K.op("act", lambda e: e.activation(out=rT[:, tb * 512:(tb + 1) * 512], in_=ps, func=AF.Copy), x=(rb,), w=(R_rT,)))
            load_scaled(wk_d[:, bc], 2)
            proj3(lambda tb, ps, rb: K.op("act", lambda e: e.activation(out=kT2[:, tb * 512:(tb + 1) * 512], in_=ps, func=AF.Copy), x=(rb,), w=(R_kT2,)))
            load_scaled(wv_d[:, bc], 3)
            proj3(lambda tb, ps, rb: K.op("act", lambda e: e.activation(out=vT[:, tb * 512:(tb + 1) * 512], in_=ps, func=AF.Copy), x=(rb,), w=(R_vT,)))
            for g4 in range(4):
                ps, rb = K.bank()
                for i4 in range(4):
                    i = g4 * 4 + i4
                    K.op("pe", lambda e, i=i, i4=i4: e.matmul(ps[:, i4 * P:(i4 + 1) * P], lhsT=vT[:, i * P:(i + 1) * P], rhs=ident_b, start=True, stop=True,
                                                            skip_group_check=True), r=(R_vT, R_cstb), w=(rb,), inc=(i4 == 3))
                K.op("act", lambda e, g4=g4: e.activation(out=vtm[:, g4 * 4:g4 * 4 + 4, :], in_=ps.rearrange("p (a b) -> p a b", a=4), func=AF.Copy),
                     x=(rb,), w=(R_vtm,))
            for tb in range(4):
                tsl = slice(tb * 512, (tb + 1) * 512)
                K.op("act", lambda e: e.activation(out=kd, in_=kT2[:, tsl], func=AF.Square, scale=rcol(0, b)), r=(R_kT2, R_rkv), w=(R_kd,))
                ps, rb = K.bank()
                K.mmg(ps, rb, [(bsum, kd)], r=(R_kd, R_cst2))
                K.op("act", lambda e: e.activation(out=T3, in_=ps, func=AF.Ln, bias=epsc[:, 2:3]), x=(rb,), w=(R_T3,), r=(R_epsc,))
                K.op("act", lambda e: e.activation(out=T3, in_=T3, func=AF.Exp, scale=-0.5), r=(R_T3,), w=(R_T3,))
                K.op("dve", lambda e: e.scalar_tensor_tensor(out=kkT[:, tsl], in0=kT2[:, tsl], scalar=rcol(0, b), in1=T3, op0=ALU.mult, op1=ALU.mult),
                     r=(R_kT2, R_T3, R_rkv), w=(R_kkT,))

            K.barrier()
            if stop == "rk_proj":
                K.op("dve", lambda e: e.tensor_copy(out=OT[:, 0, :], in_=rT), r=(R_rT,), w=(R_OT,))
                K.op("dve", lambda e: e.tensor_copy(out=OT[:, 1, :], in_=kT2), r=(R_kT2,), w=(R_OT,))
                K.op("dve", lambda e: e.tensor_copy(out=OT[:, 2, :], in_=vT), r=(R_vT,), w=(R_OT,))
                K.op("dve", lambda e: e.tensor_copy(out=OT[:, 3, :], in_=kkT), r=(R_kkT,), w=(R_OT,))
                return "dbg_y"
            for d in range(int(os.environ.get("RK_D0", "0")), NDIR):
                fwd = d == 0
                q64 = slice(64 * d, 64 * d + 64)
                mask2 = LT2 if fwd else GT2
                nmask = GTm if fwd else LTm
                K.op("dve", lambda e: e.memset(Sf, 0.0), w=(R_Sf,))
                for tb in list(range(4) if fwd else range(3, -1, -1))[:int(os.environ.get("RK_NTB", "4"))]:
                    tsl = slice(tb * 512, (tb + 1) * 512)
                    ps, rb = K.bank()
                    K.mmg(ps, rb, [(l2w[q64, 0, :], hw[q64, tsl])], r=(R_l2w, R_hw))
                    K.op("act", lambda e: e.activation(out=T3, in_=ps, func=AF.Sigmoid, bias=rcol(5 + d, b)), x=(rb,), w=(R_T3,), r=(R_rkv,))
                    K.op("act", lambda e: e.activation(out=T3, in_=T3, func=AF.Exp, scale=-E05), r=(R_T3,), w=(R_T3,))
                    ps, rb = K.bank()
                    K.mmg(ps, rb, [(l2w[q64, 1, :], ha[q64, tsl])], r=(R_l2w, R_ha))
                    K.op("act", lambda e: e.activation(out=T4, in_=ps, func=AF.Sigmoid, bias=rcol(7 + d, b)), x=(rb,), w=(R_T4,), r=(R_rkv,))
                    K.op("dve", lambda e: e.memset(T2[:, :, 0:1], 1.0), w=(R_T2,))
                    for ch in range(4):
                        K.op("dve", lambda e, ch=ch: e.tensor_tensor_scan(out=T2[:, ch, 1:129], data0=T3[:, ch * P:(ch + 1) * P], data1=ones_f,
                                                                         initial=1.0, op0=ALU.mult, op1=ALU.mult),
                             r=(R_T3, R_cst), w=(R_T2,))
                    K.op("dve", lambda e: e.tensor_copy(out=gC, in_=T2[:, :, 128]), r=(R_T2,), w=(R_gC,))
                    K.op("dve", lambda e: e.reciprocal(out=T1, in_=T2), r=(R_T2,), w=(R_T1,))
                    if fwd:
                        K.op("dve", lambda e: e.tensor_copy(out=T2[:, :, 0:1], in_=T1[:, :, 128:129]), r=(R_T1,), w=(R_T2,))
                        for ch in range(4):
                            K.op("dve", lambda e, ch=ch: e.tensor_scalar(out=T1[:, ch, 1:129], in0=T1[:, ch, 1:129], scalar1=gC[:, ch:ch + 1], scalar2=None,
                                                                        op0=ALU.mult), r=(R_T1, R_gC), w=(R_T1,))
                        K.op("dve", lambda e: e.reciprocal(out=T2[:, :, 1:129], in_=T1[:, :, 1:129]), r=(R_T1,), w=(R_T2,))
                        PA, PR, PB = T2[:, :, 0:128], T2[:, :, 1:129], T1[:, :, 1:129]
                    else:
                        PA, PR, PB = T1[:, :, 1:129], T1[:, :, 0:128], T2[:, :, 0:128]
                    v4 = lambda ap: ap.rearrange("p (a b) -> p a b", a=4)
                    K.op("dve", lambda e: e.tensor_scalar(out=T3, in0=T4, scalar1=-1.0, scalar2=rcol(1, b), op0=ALU.add, op1=ALU.mult),
                         r=(R_T4, R_rkv), w=(R_T3,))
                    K.op("dve", lambda e: e.scalar_tensor_tensor(out=kd, in0=T3, scalar=1.0, in1=kT2[:, tsl], op0=ALU.add, op1=ALU.mult),
                         r=(R_T3, R_kT2), w=(R_kd,))
                    K.op("dve", lambda e: e.tensor_tensor(out=T3, in0=T4, in1=kkT[:, tsl], op=ALU.mult), r=(R_T4, R_kkT), w=(R_T3,))
                    K.op("dve", lambda e: e.tensor_tensor(out=v4(BKf[:, 0, :]), in0=v4(T3), in1=PB, op=ALU.mult), r=(R_T3, R_T1, R_T2), w=(R_BKf,))
                    K.op("dve", lambda e: e.tensor_tensor(out=v4(BKf[:, 1, :]), in0=v4(kd), in1=PB, op=ALU.mult), r=(R_kd, R_T1, R_T2), w=(R_BKf,))
                    K.op("dve", lambda e: e.scalar_tensor_tensor(out=AR[:, :, 0, :], in0=v4(kkT[:, tsl]), scalar=-1.0, in1=PA, op0=ALU.mult, op1=ALU.mult),
                         r=(R_kkT, R_T1, R_T2), w=(R_AR,))
                    K.op("dve", lambda e: e.tensor_tensor(out=AR[:, :, 1, :], in0=v4(rT[:, tsl]), in1=PR, op=ALU.mult), r=(R_rT, R_T1, R_T2), w=(R_AR,))
                    if hit("rk_d1"):
                        K.op("dve", lambda e: e.tensor_copy(out=OT[:, 0, 0:1024], in_=AR.rearrange("p a b c -> p (a b c)")), r=(R_AR,), w=(R_OT,))
                        K.op("dve", lambda e: e.tensor_copy(out=OT[:, 1, 0:1024], in_=BKf.rearrange("p a b -> p (a b)")), r=(R_BKf,), w=(R_OT,))
                        K.op("dve", lambda e: e.tensor_copy(out=OT[:, 2, 0:512], in_=kd), r=(R_kd,), w=(R_OT,))
                        return "dbg_y"
                    for half in range(2):
                        ps, rb = K.bank()
                        for c2 in range(2):
                            ch = half * 2 + c2
                            for w_ in range(2):
                                K.op("pe", lambda e, ch=ch, c2=c2, w_=w_: e.matmul(ps[:, (c2 * 2 + w_) * P:(c2 * 2 + w_ + 1) * P], lhsT=BKf[:, w_, ch * P:(ch + 1) * P],
                                                                                 rhs=ident_b, start=True, stop=True, skip_group_check=True),
                                     r=(R_BKf, R_cstb), w=(rb,), inc=(c2 == 1 and w_ == 1))
                        K.op("act", lambda e, half=half: e.activation(out=BKt[:, half * 2:half * 2 + 2, :, :], in_=ps.rearrange("p (a b c) -> p a b c", a=2, b=2),
                                                                      func=AF.Copy), x=(rb,), w=(R_BKt,))
                    ops_, rops = K.bank(hold=True)
                    ops2, rops2 = K.bank(hold=True)
                    v2 = lambda ap: ap.rearrange("p (a b) -> p a b", a=2)

                    def chunk_gen(ch):
                        csl = slice(ch * P, (ch + 1) * P)
                        gtile = tb * 4 + ch
                        MR, KR, MPb, NNb, Wb, Ub, MO = MRc[ch], KRc[ch], MPc[ch], NNc[ch], Wbc[ch], Ubc[ch], MOc[ch]
                        R_MR, R_KR, R_MP, R_NN, R_Wb, R_Ub, R_MO = R_MRc[ch], R_KRc[ch], R_MPc[ch], R_NNc[ch], R_Wbc[ch], R_Ubc[ch], R_MOc[ch]
                        Mh = MPb[:, :, 0:128]
                        Qh = MPb[:, :, 128:256]
                        bX, rX = K.bank(); bY, rY = K.bank(); bZ, rZ = K.bank()
                        Bh = lambda h: BKf[64 * h:64 * h + 64, 0, csl]
                        Kh = lambda h: BKf[64 * h:64 * h + 64, 1, csl]
                        ARh = lambda h: AR[64 * h:64 * h + 64, ch, :, :]
                        Ah = lambda h: AR[64 * h:64 * h + 64, ch, 0, :]
                        seq = [(bX, rX, 0, Bh(0), ARh(0), 256), (bY, rY, 1, Kh(1), ARh(1), 256), (bZ, rZ, 0, Ah(0), Bh(0), 128),
                               (bX, rX, 1, Bh(1), ARh(1), 256), (bY, rY, 0, Kh(0), ARh(0), 256), (bZ, rZ, 1, Ah(1), Bh(1), 128)]
                        for si, (bk, rk_, h, l, rh, wdt) in enumerate(seq):
                            K.op("pe", lambda e, bk=bk, h=h, l=l, rh=rh, wdt=wdt: e.matmul(bk[:, h * wdt:(h + 1) * wdt], lhsT=l, rhs=rh, start=True, stop=True,
                                                                                         skip_group_check=True),
                                 r=(R_BKf, R_AR), w=(rk_,), inc=(si >= 3))
                        m2b = mask2.unsqueeze(1).to_broadcast([P, 2, 256])
                        K.op("dve", lambda e: e.tensor_tensor(out=MR, in0=v2(bX), in1=m2b, op=ALU.mult), x=(rX,), r=(R_cst2,), w=(R_MR,))
                        K.op("dve", lambda e: e.tensor_tensor(out=KR, in0=v2(bY), in1=m2b, op=ALU.mult), x=(rY,), r=(R_cst2,), w=(R_KR,))
                        ndm = NDf if fwd else NDr
                        K.op("dve", lambda e: e.tensor_tensor(out=NNb, in0=v2(bZ[:, 0:256]), in1=ndm.unsqueeze(1).to_broadcast([P, 2, P]), op=ALU.mult),
                             x=(rZ,), r=(R_cst2,), w=(R_NN,))
                        yield
                        Mfull = MR[:, :, 0:128]
                        K.op("dve", lambda e: e.tensor_tensor(out=Mh, in0=Mfull, in1=BD16.unsqueeze(1).to_broadcast([P, 2, P]), op=ALU.mult),
                             r=(R_MR, R_cst2), w=(R_MP,))
                        K.op("dve", lambda e: e.tensor_tensor(out=Qh, in0=Mh, in1=ident_b.unsqueeze(1).to_broadcast([P, 2, P]), op=ALU.add),
                             r=(R_MP, R_cstb), w=(R_MP,))
                        K.op("pool", lambda e: e.tensor_tensor(out=MO, in0=Mfull.unsqueeze(1).to_broadcast([P, 3, 2, P]),
                                                               in1=OFFS.unsqueeze(2).to_broadcast([P, 3, 2, P]), op=ALU.mult),
                             r=(R_MR, R_cst2), w=(R_MO,))
                        yield
                        for k in range(0, 4):
                            last = k == 3
                            bk, rk_ = K.bank()
                            for h in range(2):
                                if k == 0:
                                    K.op("pe", lambda e, h=h: e.matmul(bk[:, h * 256:h * 256 + 128], lhsT=NNb[:, h, :], rhs=MPb[:, h, 0:128],
                                                                       start=True, stop=True, skip_group_check=True), r=(R_NN, R_MP), w=(rk_,), inc=(h == 1))
                                elif last:
                                    K.op("pe", lambda e, h=h: e.matmul(bk[:, h * 256 + 128:h * 256 + 256], lhsT=NNb[:, h, :], rhs=MPb[:, h, 128:256],
                                                                       start=True, stop=True, skip_group_check=True), r=(R_NN, R_MP), w=(rk_,), inc=(h == 1))
                                else:
                                    K.op("pe", lambda e, h=h: e.matmul(bk[:, h * 256:(h + 1) * 256], lhsT=NNb[:, h, :], rhs=MPb[:, h, :],
                                                                       start=True, stop=True, skip_group_check=True), r=(R_NN, R_MP), w=(rk_,), inc=(h == 1))
                            if not last:
                                bk2, rk2 = K.bank()
                                for h in range(2):
                                    K.op("pe", lambda e, h=h: e.matmul(bk2[:, h * P:(h + 1) * P], lhsT=MPb[:, h, 0:128], rhs=NNb[:, h, :],
                                                                       start=True, stop=True, skip_group_check=True), r=(R_NN, R_MP), w=(rk2,), inc=(h == 1))
                            bkv = v2(bk)
                            if k > 0:
                                K.op("dve", lambda e: e.tensor_tensor(out=Qh, in0=Qh, in1=bkv[:, :, 128:256], op=ALU.add), x=(rk_,), r=(R_MP,), w=(R_MP,))
                            if not last:
                                K.op("act", lambda e: e.activation(out=Mh, in_=bkv[:, :, 0:128], func=AF.Copy), x=(rk_,), w=(R_MP,))
                                K.op("dve", lambda e: e.tensor_copy(out=NNb, in_=v2(bk2[:, 0:256])), x=(rk2,), w=(R_NN,))
                            yield
                        bk, rk_ = K.bank()
                        for h in range(2):
                            K.op("pe", lambda e, h=h: e.matmul(bk[:, h * P:(h + 1) * P], lhsT=MPb[:, h, 128:256], rhs=ident_b, start=True, stop=True, skip_group_check=True),
                                 r=(R_MP, R_cstb), w=(rk_,), inc=(h == 1))
                        K.op("act", lambda e: e.activation(out=NNb, in_=v2(bk[:, 0:256]), func=AF.Copy), x=(rk_,), w=(R_NN,))
                        yield
                        for ni in range(3):
                            lastm = ni == 2
                            bk, rk_ = K.bank()
                            for h in range(2):
                                K.op("pe", lambda e, h=h: e.matmul(bk[:, h * P:(h + 1) * P], lhsT=MO[:, ni, h, :], rhs=NNb[:, h, :], start=True, stop=True, skip_group_check=True),
                                     r=(R_MO, R_NN), w=(rk_,), inc=(h == 1))
                            K.op("act", lambda e: e.activation(out=Mh, in_=v2(bk[:, 0:256]), func=AF.Copy), x=(rk_,), w=(R_MP,))
                            yield
                            bk, rk_ = K.bank()
                            for h in range(2):
                                K.op("pe", lambda e, h=h: e.matmul(bk[:, h * P:(h + 1) * P], lhsT=MPb[:, h, 0:128], rhs=MPb[:, h, 128:256], start=True, stop=True,
                                                                   skip_group_check=True), r=(R_MP,), w=(rk_,), inc=(lastm and h == 1))
                            if not lastm:
                                for h in range(2):
                                    K.op("pe", lambda e, h=h: e.matmul(bk[:, 256 + h * P:256 + (h + 1) * P], lhsT=MPb[:, h, 128:256], rhs=MPb[:, h, 0:128], start=True, stop=True,
                                                                       skip_group_check=True), r=(R_MP,), w=(rk_,), inc=(h == 1))
                            K.op("dve", lambda e: e.tensor_tensor(out=Qh, in0=Qh, in1=v2(bk[:, 0:256]), op=ALU.add), x=(rk_,), r=(R_MP,), w=(R_MP,))
                            if not lastm:
                                K.op("dve", lambda e: e.tensor_tensor(out=NNb, in0=NNb, in1=v2(bk[:, 256:512]), op=ALU.add), x=(rk_,), r=(R_NN,), w=(R_NN,))
                            yield
                        if hit("rk_d2"):
                            K.op("dve", lambda e: e.tensor_copy(out=OT[:, 0, 0:512], in_=MR.rearrange("p a b -> p (a b)")), r=(R_MR,), w=(R_OT,))
                            K.op("dve", lambda e: e.tensor_copy(out=OT[:, 1, 0:512], in_=KR.rearrange("p a b -> p (a b)")), r=(R_KR,), w=(R_OT,))
                            K.op("dve", lambda e: e.tensor_copy(out=OT[:, 3, 0:512], in_=MPb.rearrange("p a b -> p (a b)")), r=(R_MP,), w=(R_OT,))
                            return
                        yield "seq"
                        K.op("dve", lambda e: e.tensor_scalar(out=Sb, in0=Sf, scalar1=gC[:, ch:ch + 1], scalar2=None, op0=ALU.mult), r=(R_Sf, R_gC), w=(R_Sb,))
                        bk, rk_ = K.bank()
                        for h in range(2):
                            hs = slice(64 * h, 64 * h + 64)
                            K.op("pe", lambda e, h=h, hs=hs: e.matmul(bk[:, h * 64:(h + 1) * 64], lhsT=AR[hs, ch, 0, :], rhs=Sb[hs, :], start=True, stop=False,
                                                                      skip_group_check=True), r=(R_AR, R_Sb), w=(rk_,), inc=False)
                            K.op("pe", lambda e, h=h, hs=hs: e.matmul(bk[:, h * 64:(h + 1) * 64], lhsT=KR[:, h, 0:128], rhs=vtm[:, gtile, hs], start=False, stop=True,
                                                                      skip_group_check=True), r=(R_KR, R_vtm), w=(rk_,), inc=(h == 1))
                        K.op("act", lambda e: e.activation(out=Wb, in_=v2(bk[:, 0:128]), func=AF.Copy), x=(rk_,), w=(R_Wb,))
                        bk, rk_ = K.bank()
                        for h in range(2):
                            K.op("pe", lambda e, h=h: e.matmul(bk[:, h * 64:(h + 1) * 64], lhsT=MPb[:, h, 128:256], rhs=Wb[:, h, :], start=True, stop=True,
                                                               skip_group_check=True), r=(R_MP, R_Wb), w=(rk_,), inc=(h == 1))
                        K.op("act", lambda e: e.activation(out=Ub, in_=v2(bk[:, 0:128]), func=AF.Copy), x=(rk_,), w=(R_Ub,))
                        bk, rk_ = K.bank()
                        for h in range(2):
                            hs = slice(64 * h, 64 * h + 64)
                            K.op("pe", lambda e, h=h, hs=hs: e.matmul(bk[hs, 0:64], lhsT=BKt[:, ch, 0, hs], rhs=Ub[:, h, :], start=True, stop=False, skip_group_check=True),
                                 r=(R_BKt, R_Ub), w=(rk_,), inc=False)
                            K.op("pe", lambda e, h=h, hs=hs: e.matmul(bk[hs, 0:64], lhsT=BKt[:, ch, 1, hs], rhs=vtm[:, gtile, hs], start=False, stop=True, skip_group_check=True),
                                 r=(R_BKt, R_vtm), w=(rk_,), inc=(h == 1))
                        for h in range(2):
                            hs = slice(64 * h, 64 * h + 64)
                            ob_, rob_ = (ops_, rops) if h == 0 else (ops2, rops2)
                            K.op("pe", lambda e, hs=hs: e.matmul(ob_[hs, csl], lhsT=Sb[hs, :], rhs=AR[hs, ch, 1, :], start=True, stop=False, skip_group_check=True),
                                 r=(R_Sb, R_AR), w=(rob_,), inc=False)
                            K.op("pe", lambda e, h=h, hs=hs: e.matmul(ob_[hs, csl], lhsT=Ub[:, h, :], rhs=MR[:, h, 128:256], start=False, stop=False, skip_group_check=True),
                                 r=(R_Ub, R_MR), w=(rob_,), inc=False)
                            K.op("pe", lambda e, h=h, hs=hs: e.matmul(ob_[hs, csl], lhsT=vtm[:, gtile, hs], rhs=KR[:, h, 128:256], start=False, stop=True, skip_group_check=True),
                                 r=(R_vtm, R_KR), w=(rob_,), inc=True)
                        K.op("dve", lambda e: e.scalar_tensor_tensor(out=Sf, in0=Sf, scalar=gC[:, ch:ch + 1], in1=bk[:, 0:64], op0=ALU.mult, op1=ALU.add),
                             x=(rk_,), r=(R_Sf, R_gC), w=(R_Sf,))
                        yield

                    order = list(range(4) if fwd else range(3, -1, -1))
                    gens = [chunk_gen(ch) for ch in order]
                    live = list(gens)
                    atseq = []
                    while live:
                        nl = []
                        for g in live:
                            try:
                                r_ = next(g)
                                if r_ == "seq":
                                    atseq.append(g)
                                else:
                                    nl.append(g)
                            except StopIteration:
                                pass
                        live = nl
                    for g in gens:
                        if g in atseq:
                            for _ in g:
                                pass
                    if hit("rk_d3"):
                        K.release(rops); K.release(rops2)
                        return "dbg_y"
                    K.op("act", lambda e: e.activation(out=T3[0:64, :], in_=ops_[0:64, :], func=AF.Copy), x=(rops,), w=(R_T3,))
                    K.op("act", lambda e: e.activation(out=T3[64:128, :], in_=ops2[64:128, :], func=AF.Copy), x=(rops2,), w=(R_T3,))
                    K.release(rops)
                    K.release(rops2)
                    K.op("dve", lambda e: e.scalar_tensor_tensor(out=BKt.rearrange("p a b c -> p (a b c)")[:, 0:512], in0=rT[:, tsl], scalar=rcol(2, b), in1=kd,
                                                                 op0=ALU.mult, op1=ALU.mult), r=(R_rT, R_kd, R_rkv), w=(R_BKt,))
                    ps, rb = K.bank()
                    K.mmg(ps, rb, [(bsum, BKt.rearrange("p a b c -> p (a b c)")[:, 0:512])], r=(R_BKt, R_cst2))
                    K.op("dve", lambda e: e.tensor_tensor(out=T4, in0=vT[:, tsl], in1=ps, op=ALU.mult), x=(rb,), r=(R_vT,), w=(R_T4,))
                    ps, rb = K.bank()
                    K.mmg(ps, rb, [(bmean_f, T3)], r=(R_T3, R_cst))
                    K.op("dve", lambda e: e.tensor_tensor(out=T3, in0=T3, in1=ps, op=ALU.subtract), x=(rb,), r=(R_T3,), w=(R_T3,))
                    T1f = T1.rearrange("p a b -> p (a b)")[:, 0:512]
                    T2f = T2.rearrange("p a b -> p (a b)")[:, 0:512]
                    K.op("act", lambda e: e.activation(out=T1f, in_=T3, func=AF.Square), r=(R_T3,), w=(R_T1,))
                    ps, rb = K.bank()
                    K.mmg(ps, rb, [(bmean_f, T1f)], r=(R_T1, R_cst))
                    K.op("act", lambda e: e.activation(out=T2f, in_=ps, func=AF.Ln, bias=epsc[:, 1:2]), x=(rb,), w=(R_T2,), r=(R_epsc,))
                    K.op("act", lambda e: e.activation(out=T2f, in_=T2f, func=AF.Exp, scale=-0.5), r=(R_T2,), w=(R_T2,))
                    K.op("dve", lambda e: e.tensor_tensor(out=T3, in0=T3, in1=T2f, op=ALU.mult), r=(R_T3, R_T2), w=(R_T3,))
                    K.op("dve", lambda e: e.tensor_scalar(out=T3, in0=T3, scalar1=rcol(3, b), scalar2=rcol(4, b), op0=ALU.mult, op1=ALU.add),
                         r=(R_T3, R_rkv), w=(R_T3,))
                    K.op("dve", lambda e: e.tensor_tensor(out=T3, in0=T3, in1=T4, op=ALU.add), r=(R_T3, R_T4), w=(R_T3,))
                    ps, rb = K.bank()
                    K.mmg(ps, rb, [(l2w[:, 2 + d, :], hg[:, d, tsl])], r=(R_l2w, R_hg))
                    if d == int(os.environ.get("RK_D0", "0")):
                        K.op("dve", lambda e: e.tensor_tensor(out=yT[:, b, tsl], in0=T3, in1=ps, op=ALU.mult), x=(rb,), r=(R_T3,), w=(R_yT,))
                    else:
                        K.op("dve", lambda e: e.tensor_tensor(out=T3, in0=T3, in1=ps, op=ALU.mult), x=(rb,), r=(R_T3,), w=(R_T3,))
                        K.op("dve", lambda e: e.tensor_tensor(out=yT[:, b, tsl], in0=yT[:, b, tsl], in1=T3, op=ALU.add), r=(R_T3, R_yT), w=(R_yT,))
                    if hit("rk_d4"):
                        return "dbg_y"

        if stop == "rk_y":
            return "dbg_y"
        K.barrier()
        A.top = mark_c
        usb2 = A.alloc(F32, [KC, 512])
        wo2 = wall[:, 0:KC * D].rearrange("p (k n) -> p k n", k=KC)
        K.dma("pool", wo2, wo2_d.rearrange("(b p) n -> p b n", p=P), w=(R_w[0],))
        for tb in range(4):
            for m in range(KC):
                ps, rb = K.bank()
                K.mmg(ps, rb, [(wo2[:, bb, m * P:(m + 1) * P], yT[:, bb, tb * 512:(tb + 1) * 512]) for bb in range(KC)], r=(R_w[0], R_yT))
                K.op("act", lambda e, m=m, ps=ps: e.activation(out=usb2[:, m, :], in_=ps, func=AF.Copy), x=(rb,), w=(R_usb,))
            tsl = slice(tb * 512, (tb + 1) * 512)

            def base_h2(src, rsrc, tsl=tsl):
                K.op("dve", lambda e: e.tensor_tensor(out=hT[:, :, tsl], in0=hT[:, :, tsl], in1=src, op=ALU.add), r=(rsrc, R_hT), w=(R_hT,))
            postnorm_add(1, 1, tb, 512, usb2, R_usb, base_h2)
        wi[0] = 2
        return None

    rr = rwkv()
    print("instr counts", K.cnt, "nsem", K.nsem)
    if rr == "dbg_y":
        return finish_debug(nc, es, K, A, y_d, OT, R_OT, ident_b, R_cstb, bf=True)
    if stop == "l1mix":
        return finish_debug(nc, es, K, A, y_d, hT, R_hT, ident_f, R_cst, bf=False)
    mlp(1)
    return finish_debug(nc, es, K, A, y_d, hT, R_hT, ident_f, R_cst, bf=False)


def finish_debug(nc, es, K, A, y_d, srcT, rsrc, ident, rid, bf):
    K.barrier()
    if A.top + 2 * 4096 > A.cap:
        A.top = A.cap - 2 * 4096 - 64
    ob = [A.alloc(F32, [D]) for _ in range(2)]
    R_ob = [Res("ob0"), Res("ob1")]
    R_y = Res("y")
    for tt in range(16):
        o, ro = ob[tt % 2], R_ob[tt % 2]
        for half in range(2):
            ps, rb = K.bank()
            if bf:
                K_terms = [(srcT[:, half * 4 + c4, tt * P:(tt + 1) * P], ident) for c4 in range(4)]
                for c4, (l, rh) in enumerate(K_terms):
                    K.op("pe", lambda e, l=l, rh=rh, c4=c4: e.matmul(ps[:, c4 * P:(c4 + 1) * P], lhsT=l, rhs=rh, start=True, stop=True,
                                                                     skip_group_check=True),
                         r=(rsrc, rid), w=(rb,), inc=(c4 == 3))
            else:
                for c4 in range(4):
                    c = half * 4 + c4
                    K.op("pe", lambda e, c=c, c4=c4: e.transpose(ps[:, c4 * P:(c4 + 1) * P], srcT[:, c, tt * P:(tt + 1) * P], ident),
                         r=(rsrc, rid), w=(rb,), inc=(c4 == 3))
            K.op("act" if half == 0 else "dve",
                 (lambda e, half=half, ps=ps: e.activation(out=o[:, half * 512:(half + 1) * 512], in_=ps, func=AF.Copy))
                 if half == 0 else
                 (lambda e, half=half, ps=ps: e.tensor_copy(out=o[:, half * 512:(half + 1) * 512], in_=ps)),
                 x=(rb,), w=(ro,))
        K.dma("sp", y_d[tt * P:(tt + 1) * P, :], o, r=(ro,), w=(R_y,))
    K.barrier()
    es.close()
    return nc


def make_tables(rel_table):
    rel_table = np.asarray(rel_table, np.float32)
    k = np.arange(P)[:, None]
    q = np.arange(P)[None, :]
    biasA = np.full((8, P, 896), NEG, np.float32)
    biasB = np.full((8, P, 384), NEG, np.float32)
    for r_, d in enumerate((1, 4, 16)):
        offs = (-1, 0, 1) if r_ < 2 else (0,)
        for oi, o in enumerate(offs):
            rel = k - q - P * o
            m = np.abs(rel) <= 64
            bk = t5_bucket_np(rel * d)
            col0 = (0, 384, 768)[r_] + oi * P
            for h in range(8):
                vals = rel_table[bk, h]
                biasA[h, :, col0:col0 + P] = np.where(m, vals, NEG)
    for oi, o in enumerate((-1, 0, 1)):
        rel = k - q - P * o
        m = np.abs(rel) <= 128
        bk = t5_bucket_np(rel)
        for h in range(8):
            vals = rel_table[bk, 8 + h]
            biasB[h, :, oi * P:(oi + 1) * P] = np.where(m, vals, NEG)
    return biasA, biasB


def make_consts():
    c = np.zeros((P, 4 * P), np.float32)
    c[:, 0:P] = np.eye(P)
    sw = np.zeros((P, P), np.float32)
    for m in range(P):
        sw[(m + 64) % P, m] = 1.0
    c[:, P:2 * P] = sw
    c[:, 2 * P:3 * P] = 1.0
    bm = np.zeros((P, P), np.float32)
    bm[0:64, 0:64] = 1.0 / 64
    bm[64:, 64:] = 1.0 / 64
    c[:, 3 * P:4 * P] = bm
    return c


_NC_CACHE = {}


def kernel(x, rel_table, norm_g, attn_w_in, attn_sink, attn_w_out,
           rk_mu_prev, rk_mu_next, rk_w_r, rk_w_k, rk_w_v, rk_w_o, rk_k_k, rk_k_a, rk_r_k,
           rk_gn_w, rk_gn_b, rk_w0, rk_w1, rk_w2, rk_a0, rk_a1, rk_a2, rk_g1, rk_g2,
           mlp_w1, mlp_w2, _stop=None, _cores=8):
    x = np.asarray(x, np.float32)
    biasA, biasB = make_tables(rel_table)
    gains = colvec(np.asarray(norm_g, np.float32).reshape(-1))
    sink = np.ascontiguousarray(np.broadcast_to(np.asarray(attn_sink, np.float32).reshape(1, 8), (P, 8)))
    f32 = lambda a: np.asarray(a, np.float32)
    cst2 = np.zeros((P, 11 * P), np.float32)
    pi = np.arange(P)[:, None]
    fi = np.arange(P)[None, :]
    cst2[:, 0:P] = pi < fi
    cst2[:, P:2 * P] = pi <= fi
    cst2[:, 2 * P:3 * P] = pi > fi
    cst2[:, 3 * P:4 * P] = pi >= fi
    cst2[:, 4 * P:5 * P] = (pi // 64) == (fi // 64)
    bd16 = (pi // 16) == (fi // 16)
    cst2[:, 5 * P:6 * P] = bd16
    for ni, n in enumerate((16, 32, 64)):
        cst2[:, (6 + ni) * P:(7 + ni) * P] = ((pi // (2 * n)) == (fi // (2 * n))) & ((pi // n) != (fi // n))
    cst2[:, 9 * P:10 * P] = (pi > fi) & bd16
    cst2[:, 10 * P:11 * P] = (pi < fi) & bd16
    rkv = np.concatenate([colvec(f32(v).reshape(-1)) for v in (
        rk_k_k[0], rk_k_a[0], rk_r_k[0], rk_gn_w[0], rk_gn_b[0], rk_w0[0, 0], rk_w0[0, 1], rk_a0[0, 0], rk_a0[0, 1])], axis=1)
    mu = np.concatenate([colvec(f32(m)[0, s_]) for s_ in range(6) for m in (rk_mu_prev, rk_mu_next)], axis=1)
    lora1 = np.concatenate([f32(rk_w1)[0, 0], f32(rk_w1)[0, 1], f32(rk_a1)[0, 0], f32(rk_a1)[0, 1],
                            f32(rk_g1)[0, 0], f32(rk_g1)[0, 1]], axis=1)
    common = {
        "cst2": cst2, "rkv": np.ascontiguousarray(rkv), "mu": np.ascontiguousarray(mu),
        "rk_w_r": np.ascontiguousarray(f32(rk_w_r)[0]), "rk_w_k": np.ascontiguousarray(f32(rk_w_k)[0]),
        "rk_w_v": np.ascontiguousarray(f32(rk_w_v)[0]), "rk_w_o": np.ascontiguousarray(f32(rk_w_o)[0]),
        "lora1": np.ascontiguousarray(lora1),
        "w2cat": np.ascontiguousarray(f32(rk_w2)[0].reshape(P, D)), "a2cat": np.ascontiguousarray(f32(rk_a2)[0].reshape(P, D)),
        "g2": np.ascontiguousarray(f32(rk_g2)[0]),
        "biasA": biasA, "biasB": biasB, "gains": gains, "sink": sink, "cst": make_consts(),
        "w_in": np.ascontiguousarray(np.asarray(attn_w_in, np.float32)[0]),
        "w_out": np.ascontiguousarray(np.asarray(attn_w_out, np.float32)[0]),
        "mlp_w1": np.ascontiguousarray(np.asarray(mlp_w1, np.float32)),
        "mlp_w2": np.ascontiguousarray(np.asarray(mlp_w2, np.float32)),
    }
    nc = build(_stop)
    in_maps = [dict(common, x=np.ascontiguousarray(x[b])) for b in range(_cores)]
    res = run_bass_kernel_spmd(nc, in_maps, core_ids=list(range(_cores)))
    return np.stack([np.asarray(r["y"], np.float32) for r in res.results], axis=0)
```
